# Optimizing a Trainium2 kernel written in Bass

```python
import jax, jax.numpy as jnp
from jax import lax
import numpy as np

D_MODEL = 1024
BATCH = 8
SEQ = 2048
DEPTH = 2

D_FF = 2816
CHUNK = 128
A_HEADS = 4
A_HEAD_DIM = 128
D_A = A_HEADS * A_HEAD_DIM
B_GROUPS = 8
B_GROUP_DIM = 64
D_B = B_GROUPS * B_GROUP_DIM
D_MIX = D_A + D_B
D_IN_AB = 2 * D_A + 3 * D_B
CONV_W = 3
POOL_WINDOWS = (2, 4, 8, 16)
POOL_GROUPS = len(POOL_WINDOWS)
POOL_GROUP_DIM = D_MODEL // POOL_GROUPS
N_SUB = 3
N_EVEN = (DEPTH + 1) // 2
N_ODD = DEPTH // 2
EPS = 1e-6

kernel_name = "hybrid_gmlp_shortconv_pool_macaron_adaln"


def rmsnorm(x, g):
    xf = x.astype(jnp.float32)
    y = xf * lax.rsqrt(jnp.mean(xf * xf, axis=-1, keepdims=True) + EPS)
    return (y * g.astype(jnp.float32)).astype(x.dtype)


def layernorm(x, g):
    xf = x.astype(jnp.float32)
    mu = jnp.mean(xf, axis=-1, keepdims=True)
    var = jnp.mean(jnp.square(xf - mu), axis=-1, keepdims=True)
    y = (xf - mu) * lax.rsqrt(var + EPS)
    return (y * g.astype(jnp.float32)).astype(x.dtype)


def modulate(x, g, mod):
    shift, scale, gate = jnp.split(mod, 3, axis=-1)
    h = rmsnorm(x, g) * (1.0 + scale[:, None, :]) + shift[:, None, :]
    return h, gate[:, None, :]


def swiglu(h, w_in, w_out):
    gu = h @ w_in
    g, u = jnp.split(gu, 2, axis=-1)
    return (jax.nn.silu(g) * u) @ w_out


def spatial_gating(u, v, norm_v, w_s, b_s):
    bsz, s, _ = v.shape
    n_chunks = s // CHUNK
    v = layernorm(v, norm_v)
    vc = v.reshape(bsz, n_chunks, CHUNK, A_HEADS, A_HEAD_DIM)
    mask = jnp.tril(jnp.ones((CHUNK, CHUNK), dtype=w_s.dtype))
    z = jnp.einsum('hts,bnshd->bnthd', w_s * mask[None], vc)
    z = z + jnp.transpose(b_s)[None, None, :, :, None]
    return u * z.reshape(bsz, s, D_A)


def causal_short_conv(x, w):
    s = x.shape[1]
    xp = jnp.pad(x, ((0, 0), (CONV_W - 1, 0), (0, 0)))
    y = w[0] * xp[:, 0:s]
    for k in range(1, CONV_W):
        y = y + w[k] * xp[:, k:k + s]
    return y


def mixer_ab(h, w_in, norm_v, w_s, b_s, conv_w, w_out):
    proj = h @ w_in
    u, v, bg, cg, xb = jnp.split(
        proj, [D_A, 2 * D_A, 2 * D_A + D_B, 2 * D_A + 2 * D_B], axis=-1)
    y_a = spatial_gating(jax.nn.gelu(u), jax.nn.gelu(v), norm_v, w_s, b_s)
    y_b = bg * causal_short_conv(cg * xb, conv_w)
    return jnp.concatenate([y_a, y_b], axis=-1) @ w_out


def mixer_pool(h, w_grp, scale):
    s = h.shape[1]
    cum = jnp.cumsum(h.astype(jnp.float32), axis=1)
    t = jnp.arange(s)
    outs = []
    for i, w in enumerate(POOL_WINDOWS):
        sl = slice(i * POOL_GROUP_DIM, (i + 1) * POOL_GROUP_DIM)
        cg = cum[..., sl]
        prev = jnp.pad(cg, ((0, 0), (w, 0), (0, 0)))[:, :s]
        cnt = jnp.minimum(t + 1, w).astype(jnp.float32)[None, :, None]
        p = ((cg - prev) / cnt).astype(h.dtype) - h[..., sl]
        outs.append(p @ w_grp[i])
    return jnp.concatenate(outs, axis=-1) * scale


def setup_inputs(seed: int = 0) -> dict:
    key = jax.random.key(seed)
    ks = jax.random.split(key, 20)
    f32 = jnp.float32
    nrm = lambda k, shape, s: (jax.random.normal(k, shape, f32) * s)
    x = jax.random.normal(ks[0], (BATCH, SEQ, D_MODEL), f32)
    c = jax.random.normal(ks[1], (BATCH, D_MODEL), f32)
    norm_g = 1.0 + nrm(ks[2], (DEPTH, N_SUB, D_MODEL), 0.02)
    w_mod = nrm(ks[3], (DEPTH, D_MODEL, N_SUB * 3 * D_MODEL), 0.5 * D_MODEL ** -0.5)
    b_mod = nrm(ks[4], (DEPTH, N_SUB * 3 * D_MODEL), 0.01)
    w_ffn_in = nrm(ks[5], (DEPTH, 2, D_MODEL, 2 * D_FF), D_MODEL ** -0.5)
    w_ffn_out = nrm(ks[6], (DEPTH, 2, D_FF, D_MODEL), D_FF ** -0.5)
    ab_w_in = nrm(ks[7], (N_EVEN, D_MODEL, D_IN_AB), D_MODEL ** -0.5)
    ab_norm_v = 1.0 + nrm(ks[8], (N_EVEN, D_A), 0.02)
    ab_w_s = nrm(ks[9], (N_EVEN, A_HEADS, CHUNK, CHUNK), CHUNK ** -0.5)
    ab_b_s = 1.0 + nrm(ks[10], (N_EVEN, A_HEADS, CHUNK), 0.02)
    ab_conv_w = nrm(ks[11], (N_EVEN, CONV_W, D_B), CONV_W ** -0.5)
    ab_w_out = nrm(ks[12], (N_EVEN, D_MIX, D_MODEL), D_MIX ** -0.5)
    pool_w_grp = nrm(ks[13], (N_ODD, POOL_GROUPS, POOL_GROUP_DIM, POOL_GROUP_DIM), POOL_GROUP_DIM ** -0.5)
    pool_scale = 1.0 + nrm(ks[14], (N_ODD, D_MODEL), 0.1)
    final_g = 1.0 + nrm(ks[15], (D_MODEL,), 0.02)
    return {"x": x, "c": c, "norm_g": norm_g, "w_mod": w_mod, "b_mod": b_mod,
            "w_ffn_in": w_ffn_in, "w_ffn_out": w_ffn_out,
            "ab_w_in": ab_w_in, "ab_norm_v": ab_norm_v, "ab_w_s": ab_w_s, "ab_b_s": ab_b_s,
            "ab_conv_w": ab_conv_w, "ab_w_out": ab_w_out,
            "pool_w_grp": pool_w_grp, "pool_scale": pool_scale, "final_g": final_g}


def reference(x, c, norm_g, w_mod, b_mod, w_ffn_in, w_ffn_out,
              ab_w_in, ab_norm_v, ab_w_s, ab_b_s, ab_conv_w, ab_w_out,
              pool_w_grp, pool_scale, final_g):
    c_act = jax.nn.silu(c)
    for l in range(DEPTH):
        mod = c_act @ w_mod[l] + b_mod[l]
        mod_f1, mod_mx, mod_f2 = jnp.split(mod, N_SUB, axis=-1)
        h, gate = modulate(x, norm_g[l, 0], mod_f1)
        x = x + 0.5 * gate * swiglu(h, w_ffn_in[l, 0], w_ffn_out[l, 0])
        h, gate = modulate(x, norm_g[l, 1], mod_mx)
        if l % 2 == 0:
            j = l // 2
            y = mixer_ab(h, ab_w_in[j], ab_norm_v[j], ab_w_s[j], ab_b_s[j],
                         ab_conv_w[j], ab_w_out[j])
        else:
            j = l // 2
            y = mixer_pool(h, pool_w_grp[j], pool_scale[j])
        x = x + gate * y
        h, gate = modulate(x, norm_g[l, 2], mod_f2)
        x = x + 0.5 * gate * swiglu(h, w_ffn_in[l, 1], w_ffn_out[l, 1])
    return rmsnorm(x, final_g)
```

```python
import numpy as np
import concourse.bass as bass
import concourse.mybir as mybir
from concourse.bass_utils import run_bass_kernel_spmd

F32 = mybir.dt.float32
BF16 = mybir.dt.bfloat16
AF = mybir.ActivationFunctionType
ALU = mybir.AluOpType

D = 1024
S = 2048
NC = 8
NT = 4
TT = 512
D_FF = 2816
NJ = 22
GROUPS = [list(range(0, 8)), list(range(8, 15)), list(range(15, 22))]
EPS = 1e-6
N_SUB_RUN = 6
RUN_FINAL = True


class Op:
    __slots__ = ("idx", "eng", "fn", "cdeps", "ddeps", "semkey", "pos", "seq",
                 "signal", "dma_target")


class Sched:
    ENGS = ("pe", "act", "dve", "pool", "sp")
    WINDOW = 10 ** 9

    def __init__(self):
        self.ops = []
        self.lastw = {}
        self.rd_eng = {}
        self.rd_dma = {}
        self.npos = {e: 0 for e in self.ENGS}

    def add(self, eng, fn, reads=(), writes=(), dma=None):
        ops = self.ops
        idx = len(ops)
        deps = set()
        for k in reads:
            w = self.lastw.get(k)
            if w is not None:
                deps.add(w)
        for k in writes:
            w = self.lastw.get(k)
            if w is not None:
                deps.add(w)
            r = self.rd_eng.get(k)
            if r:
                deps.update(r.values())
            r = self.rd_dma.get(k)
            if r:
                deps.update(r)
        op = Op()
        op.idx = idx
        op.eng = eng
        op.fn = fn
        op.semkey = dma
        cd = {}
        dd = []
        for d in deps:
            o = ops[d]
            if o.semkey is not None:
                dd.append(d)
            elif cd.get(o.eng, -1) < d:
                cd[o.eng] = d
        op.cdeps = cd
        op.ddeps = dd
        op.pos = self.npos[eng]
        self.npos[eng] += 1
        op.signal = False
        op.seq = 0
        op.dma_target = 0
        ops.append(op)
        for k in reads:
            if dma is not None:
                self.rd_dma.setdefault(k, []).append(idx)
            else:
                self.rd_eng.setdefault(k, {})[eng] = idx
        for k in writes:
            self.lastw[k] = idx
            self.rd_eng[k] = {}
            self.rd_dma[k] = []
        return idx

    def _skip(self, op, o):
        if o.eng == op.eng and op.semkey is None:
            if o.eng == "pe":
                return True
            if op.pos - o.pos > self.WINDOW:
                return True
        return False

    def emit(self, nc, block):
        ops = self.ops
        for op in ops:
            for e, d in op.cdeps.items():
                o = ops[d]
                if not self._skip(op, o):
                    o.signal = True
        cnt = {e: 0 for e in self.ENGS}
        dcnt = {}
        for op in ops:
            if op.semkey is not None:
                dcnt[op.semkey] = dcnt.get(op.semkey, 0) + 16
                op.dma_target = dcnt[op.semkey]
            elif op.signal:
                cnt[op.eng] += 1
                op.seq = cnt[op.eng]
        self.sig_counts = cnt
        esem = {e: nc.alloc_semaphore("s_" + e) for e in self.ENGS}
        dsem = {k: nc.alloc_semaphore("d_%d" % i) for i, k in enumerate(dcnt)}
        self.n_sems = len(esem) + len(dsem)
        by_eng = {e: [] for e in self.ENGS}
        for op in ops:
            by_eng[op.eng].append(op)

        def run(eng_name, E):
            waited = {}
            for op in by_eng[eng_name]:
                for e, d in op.cdeps.items():
                    o = ops[d]
                    if not o.signal or self._skip(op, o):
                        continue
                    if waited.get(("e", e), 0) < o.seq:
                        E.wait_ge(esem[e], o.seq)
                        waited[("e", e)] = o.seq
                for d in op.ddeps:
                    o = ops[d]
                    if waited.get(("d", o.semkey), 0) < o.dma_target:
                        E.wait_ge(dsem[o.semkey], o.dma_target)
                        waited[("d", o.semkey)] = o.dma_target
                if op.fn is None:
                    continue
                ins = op.fn(E)
                if op.semkey is not None:
                    ins.then_inc(dsem[op.semkey], 16)
                elif op.signal:
                    ins.then_inc(esem[op.eng], 1)

        if by_eng["sp"]:
            block.sync(lambda E: run("sp", E))
        if by_eng["pool"]:
            block.gpsimd(lambda E: run("pool", E))
        if by_eng["act"]:
            block.scalar(lambda E: run("act", E))
        if by_eng["dve"]:
            block.vector(lambda E: run("dve", E))
        if by_eng["pe"]:
            block.tensor(lambda E: run("pe", E))


SM_BMOD = 0
SM_G = 144
SM_FG = 192
SM_NV = 200
SM_CW = 204
SM_PS = 216
SM_BS = 224
SM_N = 736


def _blk(w, cols):
    return np.ascontiguousarray(w[:, cols].reshape(8, 128, -1).transpose(1, 0, 2))


def piece_plan():
    plan = []
    for l in range(2):
        for i in range(2):
            if i == 1:
                if l == 0:
                    plan += [("abv", 0), ("abv", 1), ("abu", 0), ("abu", 1)]
                    for hf in range(2):
                        plan += [("abb", 2, hf), ("abb", 3, hf), ("abb", 4, hf)]
                    plan += [("abo", dp) for dp in range(4)]
                else:
                    plan += [("pool",)]
            for g, ks in enumerate(GROUPS):
                for j in ks[:4]:
                    plan.append(("win", l, i, j))
                for dp in range(4):
                    plan.append(("wout", l, i, g, dp))
                for j in ks[4:]:
                    plan.append(("win", l, i, j))
    return plan


def build_pieces(inp):
    plan = piece_plan()
    out = np.zeros((len(plan), 128, 2048), np.float32)
    w_in = inp["w_ffn_in"]
    w_out = inp["w_ffn_out"]
    abw = inp["ab_w_in"][0]
    abo = inp["ab_w_out"][0]
    pw = inp["pool_w_grp"][0]
    for n, p in enumerate(plan):
        k = p[0]
        if k == "win":
            _, l, i, j = p
            w = w_in[l, i]
            a = _blk(w, slice(j * 128, (j + 1) * 128))
            b = _blk(w, slice(D_FF + j * 128, D_FF + (j + 1) * 128))
            out[n] = np.concatenate([a, b], axis=2).reshape(128, 2048)
        elif k == "wout":
            _, l, i, g, dp = p
            w = w_out[l, i]
            ks = GROUPS[g]
            blk = np.zeros((128, 2, 8, 128), np.float32)
            for dd in range(2):
                for kk, kc in enumerate(ks):
                    blk[:, dd, kk, :] = w[kc * 128:(kc + 1) * 128, (2 * dp + dd) * 128:(2 * dp + dd + 1) * 128]
            out[n] = blk.reshape(128, 2048)
        elif k == "abv":
            hp = p[1]
            out[n] = _blk(abw, slice(512 + hp * 256, 512 + (hp + 1) * 256)).reshape(128, 2048)
        elif k == "abu":
            hp = p[1]
            blk = np.stack([_blk(abw, slice((2 * hp + ff) * 128, (2 * hp + ff + 1) * 128)) for ff in range(2)], axis=1)
            out[n] = blk.reshape(128, 2048)
        elif k == "abb":
            _, sec, hf = p
            base = sec * 512
            blk = np.stack([_blk(abw, slice(base + (2 * hf + ff) * 128, base + (2 * hf + ff + 1) * 128)) for ff in range(2)], axis=1)
            out[n] = blk.reshape(128, 2048)
        elif k == "abo":
            dp = p[1]
            blk = np.stack([_blk(abo, slice((2 * dp + dd) * 128, (2 * dp + dd + 1) * 128)) for dd in range(2)], axis=1)
            out[n] = blk.reshape(128, 2048)
        elif k == "pool":
            blk = np.zeros((128, 4, 2, 256), np.float32)
            for i in range(4):
                for kc in range(2):
                    blk[:, i, kc, :] = pw[i, kc * 128:(kc + 1) * 128, :]
            out[n] = blk.reshape(128, 2048)
    return out


def build_small(inp):
    sm = np.zeros((128, SM_N), np.float32)
    sm[:, SM_BMOD:SM_BMOD + 144] = inp["b_mod"].reshape(2 * 72, 128).T
    sm[:, SM_G:SM_G + 48] = inp["norm_g"].reshape(48, 128).T
    sm[:, SM_FG:SM_FG + 8] = inp["final_g"].reshape(8, 128).T
    sm[:, SM_NV:SM_NV + 4] = inp["ab_norm_v"][0].reshape(4, 128).T
    sm[:, SM_CW:SM_CW + 12] = inp["ab_conv_w"][0].reshape(3, 4, 128).transpose(2, 0, 1).reshape(128, 12)
    sm[:, SM_PS:SM_PS + 8] = inp["pool_scale"][0].reshape(8, 128).T
    sm[:, SM_BS:SM_BS + 512] = np.broadcast_to(inp["ab_b_s"][0].reshape(1, 512), (128, 512))
    return sm


def build_wmod(inp):
    wm = inp["w_mod"]
    return np.ascontiguousarray(wm.transpose(0, 2, 1).reshape(2 * 72, 128, 1024))


def I(name, *a, **k):
    return lambda E: getattr(E, name)(*a, **k)


def build_nc(n_sub_run=N_SUB_RUN, run_final=RUN_FINAL):
    nc = bass.Bass("TRN2", target_bir_lowering=False)
    plan = piece_plan()
    xT = nc.dram_tensor("xT", [D, S], F32, kind="ExternalInput").ap()
    cb_d = nc.dram_tensor("cb", [128, D], F32, kind="ExternalInput").ap()
    small_d = nc.dram_tensor("small", [128, SM_N], F32, kind="ExternalInput").ap()
    wst_d = nc.dram_tensor("wst", [128, 512], F32, kind="ExternalInput").ap()
    wmod_d = nc.dram_tensor("wmod", [144, 128, D], F32, kind="ExternalInput").ap()
    wp_d = nc.dram_tensor("wp", [len(plan), 128, 2048], F32, kind="ExternalInput").ap()
    outT = nc.dram_tensor("outT", [D, S], F32, kind="ExternalOutput").ap()
    xT_v = xT.rearrange("(c p) t -> p c t", p=128)
    outT_v = outT.rearrange("(c p) t -> p c t", p=128)

    sb = nc.alloc_sbuf_tensor
    X = sb("X", [128, NC, S], F32)
    H = sb("H", [128, NC, S], BF16)
    AB = sb("ACTB", [128, 8, S], BF16)
    NSLOT = 7
    SL = sb("SLOTS", [128, NSLOT, 2048], BF16)
    NWM = 3
    WM = sb("WMOD", [128, NWM, D], F32)
    NTP = 8
    TPW = 516
    TP = sb("TP", [128, NTP, TPW], F32)
    NSQ = 3
    SQ = sb("SQ", [128, NSQ, TT], BF16)
    CB = sb("CB", [128, D], F32)
    JUNK = sb("JUNK", [128, D], BF16)
    SM = sb("SM", [128, SM_N], F32)
    MODT = sb("MODT", [128, 144], F32)
    AS = sb("AS", [128, 48], F32)
    GH = sb("GH", [128, 48], F32)
    WSM = sb("WSM", [128, 4, 128], BF16)
    ONES = sb("ONES", [128, 128], BF16)
    VH = sb("VHAT", [128, 4, 512], BF16)
    HALO = sb("HALO", [128, 8, 16], F32)
    INVC = sb("INVC", [128, 16], F32)
    STAT = sb("STAT", [128, 4, 16], F32)
    PMC = sb("PMC", [128, 12, 128], BF16)
    GS = sb("GS", [128, 8], F32)
    CORR = sb("CORR", [128, 4, 16], F32)
    EPSB = sb("EPSB", [128, 1], F32)
    MHALF = sb("MHALF", [128, 1], F32)
    PS = [nc.alloc_psum_tensor("ps%d" % b, [128, TT], F32) for b in range(8)]

    Sd = Sched()
    add = Sd.add

    class Rot:
        def __init__(self, n, base=0):
            self.n = n
            self.i = 0
            self.base = base

        def next(self):
            v = self.base + self.i % self.n
            self.i += 1
            return v

    rs_rot = Rot(3, 0)
    tp_rot = Rot(5, 3)
    sq_rot = Rot(NSQ)
    psA = Rot(6)
    psB = Rot(2, 6)
    ps6 = Rot(1, 6)
    ps7 = Rot(1, 7)
    win_rot = Rot(3)
    wm_rot = Rot(NWM)
    stat_rot = Rot(4)

    class Role:
        def __init__(self, hbuf, hn, abuf, an):
            self.hbuf, self.hn, self.abuf, self.an = hbuf, hn, abuf, an

    R0 = Role(H, "H", AB, "A")
    R1 = Role(AB, "A", H, "H")

    def role_of(l, s):
        return R1 if (l, s) == (1, 2) else R0

    def Hk(c, t):
        return ("H", c, t)

    def Xk(c, t):
        return ("X", c, t)

    def tsl(t):
        return slice(t * TT, (t + 1) * TT)

    piece_idx = {p: n for n, p in enumerate(plan)}
    piece_slot = {}

    def load_piece(p, slot=None, after=()):
        if p in piece_slot:
            return piece_slot[p]
        if slot is None:
            slot = win_rot.next()
        n = piece_idx[p]
        piece_slot[p] = slot
        add("pool", I("dma_start", out=SL[:, slot, :], in_=wp_d[n], max_dma_last_dim=8192),
            reads=list(after), writes=[("SL", slot)], dma=("SL", slot))
        return slot

    mod_next = [0]

    ABf = AB[:].rearrange("p a b -> p (a b)")
    ABf32 = ABf.bitcast(F32)
    Hf32 = H[:].rearrange("p a b -> p (a b)").bitcast(F32)

    def mod_dma(q):
        if q < 8:
            k = q
            ap = ABf32[:, k * 1024:(k + 1) * 1024]
            keys = [("A", k, t) for t in range(NT)]
            add("sp", I("dma_start", out=ap, in_=wmod_d[q]), writes=keys, dma=("AM", k))
        elif q < 16:
            k = q - 8
            ap = Hf32[:, k * 1024:(k + 1) * 1024]
            keys = [("H", k, t) for t in range(NT)]
            add("sp", I("dma_start", out=ap, in_=wmod_d[q]), writes=keys, dma=("HM", k))
        else:
            s = wm_rot.next()
            ap = WM[:, s, :]
            keys = [("WM", s)]
            add("sp", I("dma_start", out=ap, in_=wmod_d[q]), writes=keys, dma=("WM", s))
        return ap, keys

    mod_staged = {}

    def mod_step():
        q = mod_next[0]
        if q >= 144:
            return
        mod_next[0] += 1
        if q in mod_staged:
            ap, keys = mod_staged.pop(q)
        else:
            ap, keys = mod_dma(q)
        add("dve", I("scalar_tensor_tensor", out=JUNK[:], in0=ap, scalar=1.0, in1=CB[:],
                     op0=ALU.mult, op1=ALU.mult, accum_out=MODT[:, q:q + 1]),
            reads=keys + ["CB"], writes=[("MODT", q), "JUNK"])

    def mod_until(q_end):
        while mod_next[0] < q_end:
            mod_step()

    def mod_finalize_norm(l, s):
        base = l * 72 + s * 24
        mod_until(base + 16)
        o = (l * 3 + s) * 8
        add("dve", I("tensor_tensor", out=MODT[:, base:base + 16], in0=MODT[:, base:base + 16],
                     in1=SM[:, SM_BMOD + base:SM_BMOD + base + 16], op=ALU.add),
            reads=[("MODT", base + i) for i in range(16)] + ["SM"], writes=[("MODF", l, s)])
        add("dve", I("scalar_tensor_tensor", out=AS[:, o:o + 8], in0=MODT[:, base + 8:base + 16], scalar=1.0,
                     in1=SM[:, SM_G + o:SM_G + o + 8], op0=ALU.add, op1=ALU.mult),
            reads=[("MODF", l, s), "SM"], writes=[("AS", l, s)])

    def mod_finalize_gate(l, s):
        base = l * 72 + s * 24
        mod_until(base + 24)
        o = (l * 3 + s) * 8
        gsl = slice(base + 16, base + 24)
        add("dve", I("tensor_tensor", out=MODT[:, gsl], in0=MODT[:, gsl],
                     in1=SM[:, SM_BMOD + base + 16:SM_BMOD + base + 24], op=ALU.add),
            reads=[("MODT", base + 16 + i) for i in range(8)] + ["SM"], writes=[("MODG", l, s)])
        if s != 1:
            add("dve", I("tensor_scalar", out=GH[:, o:o + 8], in0=MODT[:, gsl], scalar1=0.5, scalar2=None, op0=ALU.mult),
                reads=[("MODG", l, s)], writes=[("GH", l, s)])
        elif l == 0:
            add("dve", I("tensor_copy", GH[:, o:o + 8], MODT[:, gsl]), reads=[("MODG", l, s)], writes=[("GH", l, s)])
        else:
            add("dve", I("tensor_tensor", out=GS[:], in0=MODT[:, gsl], in1=SM[:, SM_PS:SM_PS + 8], op=ALU.mult),
                reads=[("MODG", l, s), "SM"], writes=["GS"])
            for i in range(4):
                add("dve", I("tensor_scalar", out=GH[:, o + 2 * i:o + 2 * i + 2], in0=GS[:, 2 * i:2 * i + 2],
                             scalar1=1.0 / (2 ** (i + 1)), scalar2=None, op0=ALU.mult),
                    reads=["GS"], writes=[("GH", l, s)] if i == 3 else [("GHpart", i)])

    def stepper(gl):
        gens = [[iter(g), w, True] for g, w in gl]
        while any(a for _, _, a in gens):
            for ent in gens:
                if not ent[2]:
                    continue
                for _ in range(ent[1]):
                    try:
                        next(ent[0])
                    except StopIteration:
                        ent[2] = False
                        break

    def run_stages(stages, weights, drop_last=False):
        nst = len(stages)
        for step in range(NT + nst - 1 - (1 if drop_last else 0)):
            gl = []
            for si, st in enumerate(stages):
                t = step - si
                if 0 <= t < NT:
                    gl.append((st(t), weights[si]))
            stepper(gl)

    def drain(g):
        for _ in g:
            pass

    def stats_gen(t, res, rot=None, sq_pool=False):
        b = (rot or psB).next()
        for c in range(NC):
            q = sq_rot.next()
            if sq_pool:
                add("pool", I("tensor_tensor", out=SQ[:, q, :], in0=X[:, c, tsl(t)], in1=X[:, c, tsl(t)], op=ALU.mult),
                    reads=[Xk(c, t)], writes=[("SQ", q)])
            else:
                add("act", I("activation", out=SQ[:, q, :], in_=X[:, c, tsl(t)], func=AF.Square),
                    reads=[Xk(c, t)], writes=[("SQ", q)])
            add("pe", I("matmul", PS[b][:], ONES[:], SQ[:, q, :], start=(c == 0), stop=(c == NC - 1)),
                reads=[("SQ", q), "ONES"], writes=[("ps", b)])
            yield
        res.append(rstd_from(b))
        yield

    def rstd_from(b):
        sd = tp_rot.next()
        add("act", I("activation", out=TP[:, sd, 0:TT], in_=PS[b][:], func=AF.Ln, scale=1.0 / D, bias=EPSB[:, 0:1]),
            reads=[("ps", b), "EPSB"], writes=[("TP", sd)])
        rs = rs_rot.next()
        add("act", I("activation", out=TP[:, rs, 0:TT], in_=TP[:, sd, 0:TT], func=AF.Exp, scale=-0.5),
            reads=[("TP", sd)], writes=[("TP", rs)])
        return rs

    SQ8 = [(SQ[:, 0, :], ("SQ", 0)), (SQ[:, 1, :], ("SQ", 1)), (SQ[:, 2, :], ("SQ", 2)),
           (VH[:, 0, :], ("VH", 0)), (VH[:, 1, :], ("VH", 1)), (VH[:, 2, :], ("VH", 2)), (VH[:, 3, :], ("VH", 3)),
           (JUNK[:, 0:TT], "JUNK")]

    def square1(t, c):
        ap, k = SQ8[c]
        add("act", I("activation", out=ap, in_=X[:, c, tsl(t)], func=AF.Square), reads=[Xk(c, t)], writes=[k])

    def squares8(t):
        for c in range(NC):
            square1(t, c)

    def tail_with_squares(tg, t_sq):
        c = 4
        for _ in tg:
            square1(t_sq, c)
            c += 1
            yield

    def ones8(t):
        b = psB.next()
        for c in range(NC):
            ap, k = SQ8[c]
            add("pe", I("matmul", PS[b][:], ONES[:], ap, start=(c == 0), stop=(c == NC - 1)), reads=[k, "ONES"], writes=[("ps", b)])
        return rstd_from(b)

    def boundary(tail_fn, head_apply):
        pre_sq = head_apply is not None
        for step in range(NT + (1 if head_apply else 0)):
            tt = step if step < NT else None
            th = step - 1 if (head_apply and step >= 1) else None
            tg = tail_fn(tt) if tt is not None else None
            if th is not None and not (pre_sq and tg is None):
                squares8(th)
            if tg is not None:
                for _ in range(4 if (pre_sq and tt == NT - 1) else 2):
                    next(tg)
            ag = None
            if th is not None:
                rs = ones8(th)
                ag = head_apply(th, rs)
            gl = []
            if tg is not None:
                if pre_sq and tt == NT - 1:
                    for c in range(4):
                        square1(tt, c)
                    gl.append((tail_with_squares(tg, tt), 1))
                else:
                    gl.append((tg, 1))
            if ag is not None:
                gl.append((ag, 3))
            stepper(gl)

    def apply_gen(l, s, t, rs, kind, pool_chunks=()):
        role = role_of(l, s)
        o = (l * 3 + s) * 8
        sh = l * 72 + s * 24
        for c in range(NC):
            tm = tp_rot.next()
            add("pool" if c in pool_chunks else "dve",
                I("tensor_tensor", out=TP[:, tm, 0:TT], in0=X[:, c, tsl(t)], in1=TP[:, rs, 0:TT], op=ALU.mult),
                reads=[Xk(c, t), ("TP", rs)], writes=[("TP", tm)])
            if kind == "H":
                add("act", I("activation", out=role.hbuf[:, c, tsl(t)], in_=TP[:, tm, 0:TT], func=AF.Identity,
                             scale=AS[:, o + c:o + c + 1], bias=MODT[:, sh + c:sh + c + 1]),
                    reads=[("TP", tm), ("AS", l, s), ("MODF", l, s)], writes=[(role.hn, c, t)])
            else:
                add("act", I("activation", out=HFv[:, c, 16:16 + TT], in_=TP[:, tm, 0:TT], func=AF.Identity,
                             scale=AS[:, o + c:o + c + 1], bias=MODT[:, sh + c:sh + c + 1]),
                    reads=[("TP", tm), ("AS", l, s), ("MODF", l, s), INDONE] + ALLH, writes=[("HF", c)])
            yield

    def head_gen(l, s, t, kind="H", rot=None):
        res = []
        yield from stats_gen(t, res, rot)
        yield from apply_gen(l, s, t, res[0], kind)

    TD = ("TAILDONE",)

    INDONE = ("INDONE",)

    def outproj_gen(l, s, t, slot_of, nk, rot, mark=False):
        o = (l * 3 + s) * 8
        role = role_of(l, s)
        for dch in range(NC):
            b = rot.next()
            slot = slot_of(dch // 2)
            dd = dch % 2
            for kk in range(nk):
                wr = [("ps", b)]
                if mark and t == NT - 1 and dch == NC - 1 and kk == nk - 1:
                    wr.append(TD)
                add("pe", I("matmul", PS[b][:], SL[:, slot, (dd * 8 + kk) * 128:(dd * 8 + kk + 1) * 128], role.abuf[:, kk, tsl(t)],
                            start=(kk == 0), stop=(kk == nk - 1)),
                    reads=[("SL", slot), (role.an, kk, t)], writes=wr)
            add("dve", I("scalar_tensor_tensor", out=X[:, dch, tsl(t)], in0=PS[b][:], scalar=GH[:, o + dch:o + dch + 1],
                         in1=X[:, dch, tsl(t)], op0=ALU.mult, op1=ALU.add),
                reads=[("ps", b), ("GH", l, s), Xk(dch, t)], writes=[Xk(dch, t)])
            yield

    def ffn_inproj(l, i, j, kk, t, slot, mark=False):
        role = role_of(l, 2 * i)
        bg = psA.next()
        bu = psA.next()
        for c in range(NC):
            add("pe", I("matmul", PS[bg][:], SL[:, slot, c * 256:c * 256 + 128], role.hbuf[:, c, tsl(t)], start=(c == 0), stop=(c == NC - 1)),
                reads=[("SL", slot), (role.hn, c, t)], writes=[("ps", bg)])
        for c in range(NC):
            wr = [("ps", bu)]
            if mark and c == NC - 1:
                wr.append(INDONE)
            add("pe", I("matmul", PS[bu][:], SL[:, slot, c * 256 + 128:c * 256 + 256], role.hbuf[:, c, tsl(t)], start=(c == 0), stop=(c == NC - 1)),
                reads=[("SL", slot), (role.hn, c, t)], writes=wr)
        sg = tp_rot.next()
        add("act", I("activation", out=TP[:, sg, 0:TT], in_=PS[bg][:], func=AF.Silu), reads=[("ps", bg)], writes=[("TP", sg)])
        add("dve", I("tensor_tensor", out=role.abuf[:, kk, tsl(t)], in0=TP[:, sg, 0:TT], in1=PS[bu][:], op=ALU.mult),
            reads=[("TP", sg), ("ps", bu)], writes=[(role.an, kk, t)])

    inproj_done = set()

    def early_wave_gen(l, i, npieces, ntiles):
        ks = GROUPS[0]
        for kk in range(npieces):
            slot = load_piece(("win", l, i, ks[kk]))
            for t in range(ntiles):
                ffn_inproj(l, i, ks[kk], kk, t, slot)
                inproj_done.add((l, i, ks[kk], t))
                yield

    def ffn_prefetch(l, i, n=3):
        for j in GROUPS[0][:n]:
            load_piece(("win", l, i, j))

    def ffn_body(l, s, i, mod_bg_until, wave_cb=None):
        for g, ks in enumerate(GROUPS):
            nk = len(ks)
            kk0 = 0
            if g == 0 and wave_cb is not None:
                slots = [load_piece(("win", l, i, j)) for j in ks[:3]]
                for t in range(NT):
                    for kk in range(3):
                        ffn_inproj(l, i, ks[kk], kk, t, slots[kk])
                        if mod_next[0] < mod_bg_until:
                            mod_step()
                    wave_cb(t)
                kk0 = 3
            for kk in range(kk0, nk):
                j = ks[kk]
                if kk == 4:
                    for dp in range(4):
                        load_piece(("wout", l, i, g, dp), 3 + dp)
                slot = load_piece(("win", l, i, j))
                for t in range(NT):
                    if (l, i, j, t) in inproj_done:
                        continue
                    ffn_inproj(l, i, j, kk, t, slot, mark=((l, s) == (1, 0) and g == 2 and kk == nk - 1 and t == NT - 1))
                    if mod_next[0] < mod_bg_until:
                        mod_step()
            if g == 0:
                mod_finalize_gate(l, s)

            if g < 2:
                for t in range(NT):
                    drain(outproj_gen(l, s, t, lambda dp: 3 + dp, nk, psB))

    def ffn_tail(l, s):
        return lambda t: outproj_gen(l, s, t, lambda dp: 3 + dp, len(GROUPS[2]), psA, mark=(l == 1 and s == 0))

    AB_OSLOT = {0: 3, 1: 4, 2: 5, 3: 6}

    def ab_prefetch():
        load_piece(("abv", 0), win_rot.next())
        load_piece(("abv", 1), win_rot.next())
        load_piece(("abu", 0), win_rot.next())

    def ab_body(l, s):
        wt = tp_rot.next()
        add("sp", I("dma_start", out=TP[:, wt, 0:512], in_=wst_d), writes=[("TP", wt)], dma="WST")
        add("pool", I("affine_select", out=TP[:, wt, 0:512].rearrange("p (h t) -> p h t", h=4),
                      in_=TP[:, wt, 0:512].rearrange("p (h t) -> p h t", h=4),
                      pattern=[[0, 4], [1, 128]], compare_op=ALU.is_ge, fill=0.0, base=0, channel_multiplier=-1),
            reads=[("TP", wt)], writes=[("TP", wt)])
        add("dve", I("tensor_copy", WSM[:].rearrange("p h t -> p (h t)"), TP[:, wt, 0:512]), reads=[("TP", wt)], writes=["WSM"])
        vs = [piece_slot[("abv", 0)], piece_slot[("abv", 1)]]
        us = [piece_slot[("abu", 0)], load_piece(("abu", 1), 3)]
        bsl = {0: (4, 5, 6)}
        for si, sec in enumerate((2, 3, 4)):
            load_piece(("abb", sec, 0), bsl[0][si])
        for t in range(NT):
            pend = {}

            def v_stats(n):
                tok = slice(t * TT + n * 128, t * TT + (n + 1) * 128)
                b = psA.next()
                for hp in range(2):
                    for c in range(NC):
                        add("pe", I("matmul", PS[b][:, hp * 256:(hp + 1) * 256], H[:, c, tok], SL[:, vs[hp], c * 256:(c + 1) * 256],
                                    start=(c == 0), stop=(c == NC - 1)),
                            reads=[("SL", vs[hp]), Hk(c, t)], writes=[("ps", b)])
                gv = tp_rot.next()
                add("act", I("activation", out=TP[:, gv, 0:TT], in_=PS[b][:], func=AF.Gelu_apprx_tanh),
                    reads=[("ps", b)], writes=[("TP", gv)])
                st = stat_rot.next()
                add("dve", I("bn_stats", out=STAT[:, st, 0:6], in_=TP[:, gv, 0:TT]), reads=[("TP", gv)], writes=[("ST", st, 0)])
                add("dve", I("bn_aggr", out=STAT[:, st, 6:8], in_=STAT[:, st, 0:6]), reads=[("ST", st, 0)], writes=[("ST", st, 1)])
                add("pool", I("tensor_scalar", out=STAT[:, st, 8:9], in0=STAT[:, st, 7:8], scalar1=EPS, scalar2=None, op0=ALU.add),
                    reads=[("ST", st, 1)], writes=[("ST", st, 2)])
                add("pool", I("tensor_tensor", out=STAT[:, st, 9:10], in0=STAT[:, st, 8:9], in1=MHALF[:, 0:1], op=ALU.pow),
                    reads=[("ST", st, 2), "MHALF"], writes=[("ST", st, 3)])
                pend[n] = (gv, st)

            def v_norm(n):
                gv, st = pend[n]
                add("dve", I("tensor_scalar", out=VH[:, n, :], in0=TP[:, gv, 0:TT], scalar1=STAT[:, st, 6:7],
                             scalar2=STAT[:, st, 9:10], op0=ALU.subtract, op1=ALU.mult),
                    reads=[("TP", gv), ("ST", st, 1), ("ST", st, 3)], writes=[("VH", n)])

            v_stats(0)
            v_stats(1)
            v_norm(0)
            v_stats(2)
            v_norm(1)
            v_stats(3)
            v_norm(2)
            v_norm(3)

            gus = {}

            def u_part(hd, gu):
                bu = psA.next()
                usl = us[hd // 2]
                ff = hd % 2
                for c in range(NC):
                    add("pe", I("matmul", PS[bu][:], SL[:, usl, (ff * 8 + c) * 128:(ff * 8 + c + 1) * 128], H[:, c, tsl(t)],
                                start=(c == 0), stop=(c == NC - 1)),
                        reads=[("SL", usl), Hk(c, t)], writes=[("ps", bu)])
                add("act", I("activation", out=TP[:, gu, 0:TT], in_=PS[bu][:], func=AF.Gelu_apprx_tanh),
                    reads=[("ps", bu)], writes=[("TP", gu)])
                gus[hd] = gu

            def z_part(hd):
                gu = gus[hd]
                bz = psA.next()
                for n in range(4):
                    add("pe", I("matmul", PS[bz][:, n * 128:(n + 1) * 128], VH[:, n, hd * 128:(hd + 1) * 128], WSM[:, hd, :],
                                start=True, stop=True),
                        reads=[("VH", n), "WSM"], writes=[("ps", bz)])
                z = tp_rot.next()
                add("dve", I("scalar_tensor_tensor", out=TP[:, z, 0:TT].rearrange("p (n k) -> p n k", n=4),
                             in0=PS[bz][:].rearrange("p (n k) -> p n k", n=4), scalar=SM[:, SM_NV + hd:SM_NV + hd + 1],
                             in1=SM[:, SM_BS + hd * 128:SM_BS + (hd + 1) * 128].unsqueeze(1).broadcast_to([128, 4, 128]),
                             op0=ALU.mult, op1=ALU.add),
                    reads=[("ps", bz), "SM"], writes=[("TP", z)])
                add("dve", I("tensor_tensor", out=AB[:, hd, tsl(t)], in0=TP[:, z, 0:TT], in1=TP[:, gu, 0:TT], op=ALU.mult),
                    reads=[("TP", z), ("TP", gu)], writes=[("A", hd, t)])

            u_part(0, 0)
            u_part(1, 1)
            u_part(2, 2)
            u_part(3, tp_rot.next())
            z_part(0)
            z_part(1)
            z_part(2)
            z_part(3)
        bsl[1] = (vs[0], vs[1], us[0])
        for hf in range(2):
            sl3 = bsl[hf]
            if hf == 1:
                for si, sec in enumerate((2, 3, 4)):
                    load_piece(("abb", sec, 1), sl3[si])
                for dp in range(4):
                    load_piece(("abo", dp), AB_OSLOT[dp])
            for ff in range(2):
                q = 2 * hf + ff
                prevP = None
                for t in range(NT):
                    banks = [psA.next() for _ in range(3)]
                    for si in range(3):
                        for c in range(NC):
                            add("pe", I("matmul", PS[banks[si]][:], SL[:, sl3[si], (ff * 8 + c) * 128:(ff * 8 + c + 1) * 128], H[:, c, tsl(t)],
                                        start=(c == 0), stop=(c == NC - 1)),
                                reads=[("SL", sl3[si]), Hk(c, t)], writes=[("ps", banks[si])])
                    cgs = tp_rot.next()
                    add("act", I("activation", out=TP[:, cgs, 0:TT], in_=PS[banks[1]][:], func=AF.Copy),
                        reads=[("ps", banks[1])], writes=[("TP", cgs)])
                    P = tp_rot.next()
                    if prevP is None:
                        add("dve", I("memset", TP[:, P, 0:2], 0.0), writes=[("TPh", P), ("TP", P)])
                    else:
                        add("act", I("activation", out=TP[:, P, 0:2], in_=TP[:, prevP, 512:514], func=AF.Copy),
                            reads=[("TP", prevP)], writes=[("TPh", P), ("TP", P)])
                    add("dve", I("tensor_tensor", out=TP[:, P, 2:514], in0=TP[:, cgs, 0:TT], in1=PS[banks[2]][:], op=ALU.mult),
                        reads=[("TP", cgs), ("ps", banks[2])], writes=[("TP", P)])
                    acc = tp_rot.next()
                    add("act", I("activation", out=TP[:, acc, 0:TT], in_=TP[:, P, 2:514], func=AF.Identity,
                                 scale=SM[:, SM_CW + 2 * 4 + q:SM_CW + 2 * 4 + q + 1]),
                        reads=[("TP", P), "SM"], writes=[("TP", acc)])
                    add("dve", I("scalar_tensor_tensor", out=TP[:, acc, 0:TT], in0=TP[:, P, 1:513],
                                 scalar=SM[:, SM_CW + 1 * 4 + q:SM_CW + 1 * 4 + q + 1], in1=TP[:, acc, 0:TT], op0=ALU.mult, op1=ALU.add),
                        reads=[("TP", P), ("TPh", P), ("TP", acc), "SM"], writes=[("TP", acc)])
                    add("dve", I("scalar_tensor_tensor", out=TP[:, acc, 0:TT], in0=TP[:, P, 0:512],
                                 scalar=SM[:, SM_CW + 0 * 4 + q:SM_CW + 0 * 4 + q + 1], in1=TP[:, acc, 0:TT], op0=ALU.mult, op1=ALU.add),
                        reads=[("TP", P), ("TPh", P), ("TP", acc), "SM"], writes=[("TP", acc)])
                    add("dve", I("tensor_tensor", out=AB[:, 4 + q, tsl(t)], in0=TP[:, acc, 0:TT], in1=PS[banks[0]][:], op=ALU.mult),
                        reads=[("TP", acc), ("ps", banks[0])], writes=[("A", 4 + q, t)])
                    prevP = P
            if hf == 0:
                pass

    def ab_tail(l, s):
        return lambda t: outproj_gen(l, s, t, lambda dp: AB_OSLOT[dp], 8, psA)

    Hf = H[:].rearrange("p a b -> p (a b)")
    HFv = Hf32[:, 0:4224].rearrange("p (c k) -> p c k", k=528)
    SAv = Hf32[:, 4224:5280].rearrange("p (c k) -> p c k", k=528)
    SBv = Hf32[:, 5280:6336].rearrange("p (c k) -> p c k", k=528)
    SCv = Hf32[:, 6336:7392].rearrange("p (c k) -> p c k", k=528)
    PPs = Hf[:, 14784:15808].rearrange("p (c t) -> p c t", t=TT)
    ALLH = [("H", k, t) for k in range(8) for t in range(NT)]
    WENG = "pool"

    def pm_load():
        load_piece(("pool",), win_rot.next())

    WMb = WM[:].rearrange("p a b -> p (a b)").bitcast(BF16)
    GTv = [WMb[:, j * 1024:(j + 1) * 1024] for j in range(5)]
    GTK = [[("GT", j), ("WM", j // 2)] for j in range(5)]
    BANDv = [PMC[:, i, :] for i in range(4)]
    BPREVv = [PMC[:, 4 + i, :] for i in range(4)]
    BAND0v = [PMC[:, 8 + i, :] for i in range(4)]

    INVC_KEYS = [("INVC", k) for k in range(16)]

    def pm_prefetch():
        pass

    def pm_consts():
        for k in range(16):
            add("dve", I("memset", INVC[:, k:k + 1], 1.0 / (k + 1)), writes=[("INVC", k)])
        onesf = TP[:, 0, 0:128]
        eye = TP[:, 1, 0:128]
        add("dve", I("memset", onesf, 1.0), writes=[("TP", 0)])
        add("pool", I("affine_select", out=eye, in_=onesf, pattern=[[1, 128]], compare_op=ALU.is_equal, fill=0.0,
                      base=0, channel_multiplier=-1), reads=[("TP", 0)], writes=[("TP", 1)])
        for i in range(4):
            w = 2 ** (i + 1)
            b1 = tp_rot.next()
            B1 = TP[:, b1, 0:128]
            add("pool", I("affine_select", out=B1, in_=onesf, pattern=[[1, 128]], compare_op=ALU.is_ge, fill=0.0,
                          base=0, channel_multiplier=-1), reads=[("TP", 0)], writes=[("TP", b1)])
            add("pool", I("affine_select", out=B1, in_=B1, pattern=[[-1, 128]], compare_op=ALU.is_ge, fill=0.0,
                          base=w - 1, channel_multiplier=1), reads=[("TP", b1)], writes=[("TP", b1)])
            add("dve", I("scalar_tensor_tensor", out=BANDv[i], in0=eye, scalar=-float(w), in1=B1, op0=ALU.mult, op1=ALU.add),
                reads=[("TP", 1), ("TP", b1)], writes=[("BAND", i)])
            add("pool", I("affine_select", out=BPREVv[i], in_=onesf, pattern=[[-1, 128]], compare_op=ALU.is_ge, fill=0.0,
                          base=-(129 - w), channel_multiplier=1), reads=[("TP", 0)], writes=[("BPREV", i)])
            v = tp_rot.next()
            V = TP[:, v, 0:16]
            add("dve", I("memset", V, 0.0), writes=[("TP", v)])
            for t in range(w - 1):
                add("dve", I("memset", TP[:, v, t:t + 1], float(w - 1 - t)), reads=[("TP", v)], writes=[("TPc", v, t)])
            add("dve", I("tensor_tensor", out=V, in0=V, in1=TP[:, 1, 0:16], op=ALU.mult),
                reads=[("TP", v), ("TP", 1)] + [("TPc", v, t) for t in range(w - 1)], writes=[("TP", v)])
            add("dve", I("tensor_tensor", out=BAND0v[i][:, 0:16], in0=V, in1=BANDv[i][:, 0:16], op=ALU.add),
                reads=[("TP", v), ("BAND", i)], writes=[("BAND0", i)])
            add("dve", I("tensor_copy", BAND0v[i][:, 16:128], BANDv[i][:, 16:128]), reads=[("BAND", i)], writes=[("BAND0b", i)])
            add("dve", I("tensor_scalar", out=CORR[:, i, :], in0=INVC[:], scalar1=-1.0 / w, scalar2=None, op0=ALU.add),
                reads=INVC_KEYS, writes=[("CORR", i)])

    def pm_core_gen(l, s, t):
        o = (l * 3 + s) * 8
        psl = piece_slot[("pool",)]
        for n in range(4):
            g = 4 * t + n
            j = g % 5
            tok = slice(t * TT + n * 128, t * TT + (n + 1) * 128)
            banks = [psA.next(), psA.next()]
            for i in range(4):
                bk = banks[i // 2]
                col = (i % 2) * 256
                for kc in range(2):
                    add("pe", I("matmul", PS[bk][:, col:col + 256], H[:, 2 * i + kc, tok], SL[:, psl, (i * 2 + kc) * 256:(i * 2 + kc + 1) * 256],
                                start=(kc == 0), stop=(kc == 1)),
                        reads=[("SL", psl), Hk(2 * i + kc, t)], writes=[("ps", bk)])
            for hh in range(2):
                add("act", I("activation", out=GTv[j][:, hh * 512:(hh + 1) * 512], in_=PS[banks[hh]][:], func=AF.Copy),
                    reads=[("ps", banks[hh])], writes=[("GTh", j, hh)] + GTK[j])
            yield
        for dch in range(NC):
            i = dch // 2
            w = 2 ** (i + 1)
            bk = psA.next()
            for n in range(4):
                g = 4 * t + n
                j = g % 5
                first = (g == 0)
                rd = [("GTh", j, dch // 4)] + GTK[j]
                add("pe", I("matmul", PS[bk][:, n * 128:(n + 1) * 128], GTv[j][:, dch * 128:(dch + 1) * 128],
                            BAND0v[i] if first else BANDv[i], start=True, stop=first),
                    reads=rd + ([("BAND0", i), ("BAND0b", i)] if first else [("BAND", i)]), writes=[("ps", bk)])
                if not first:
                    jp = (g - 1) % 5
                    add("pe", I("matmul", PS[bk][:, n * 128:(n + 1) * 128], GTv[jp][:, dch * 128:(dch + 1) * 128], BPREVv[i],
                                start=False, stop=True),
                        reads=[("GTh", jp, dch // 4), ("BPREV", i)] + GTK[jp], writes=[("ps", bk)])
            add("dve", I("scalar_tensor_tensor", out=X[:, dch, tsl(t)], in0=PS[bk][:], scalar=GH[:, o + dch:o + dch + 1],
                         in1=X[:, dch, tsl(t)], op0=ALU.mult, op1=ALU.add),
                reads=[("ps", bk), ("GH", l, s), Xk(dch, t)], writes=[Xk(dch, t)])
            if t == 0:
                tmpi = tp_rot.next()
                add("dve", I("tensor_tensor", out=TP[:, tmpi, 0:w - 1], in0=PS[bk][:, 0:w - 1], in1=CORR[:, i, 0:w - 1], op=ALU.mult),
                    reads=[("ps", bk), ("CORR", i)], writes=[("TP", tmpi)])
                add("dve", I("scalar_tensor_tensor", out=X[:, dch, 0:w - 1], in0=TP[:, tmpi, 0:w - 1], scalar=GS[:, dch:dch + 1],
                             in1=X[:, dch, 0:w - 1], op0=ALU.mult, op1=ALU.add),
                    reads=[("TP", tmpi), "GS", Xk(dch, t)], writes=[Xk(dch, t)])
            yield

    def pm_stage2(tail_fn, head_fn):
        for k in range(NT + 3):
            gl = []
            if k < NT:
                gl.append((tail_fn(k), 1))
            if 0 <= k - 1 < NT:
                gl.append((head_lag_gen(1, 1, k - 1, ps6, SQ_PM, (), True), 2))
            if 0 <= k - 2 < NT:
                gl.append((pm_core_gen(1, 1, k - 2), 2))
            h = k - 3
            if head_fn is not None and 0 <= h < NT:
                gl.append((head_fn(h), 2))
                if h == NT - 1:
                    gl.append((early_wave_gen(1, 1, 2, 3), 1))
            stepper(gl)

    SQ_PM = [(VH[:, 1, :], ("VH", 1)), (VH[:, 2, :], ("VH", 2)), (VH[:, 3, :], ("VH", 3)), (JUNK[:, 0:TT], "JUNK")]
    SQ_HD = [(SQ[:, 0, :], ("SQ", 0)), (SQ[:, 1, :], ("SQ", 1)), (SQ[:, 2, :], ("SQ", 2)), (VH[:, 0, :], ("VH", 0))]

    def stats_lag_gen(t, res, rot, tiles, lag=3, sq_pool=False):
        b = rot.next()
        n = len(tiles)
        for c in range(NC + lag):
            if c < NC:
                ap, k = tiles[c % n]
                if sq_pool:
                    add("pool", I("tensor_tensor", out=ap, in0=X[:, c, tsl(t)], in1=X[:, c, tsl(t)], op=ALU.mult),
                        reads=[Xk(c, t)], writes=[k])
                else:
                    add("act", I("activation", out=ap, in_=X[:, c, tsl(t)], func=AF.Square), reads=[Xk(c, t)], writes=[k])
            cc = c - lag
            if cc >= 0:
                ap, k = tiles[cc % n]
                add("pe", I("matmul", PS[b][:], ONES[:], ap, start=(cc == 0), stop=(cc == NC - 1)), reads=[k, "ONES"], writes=[("ps", b)])
            yield
        res.append(rstd_from(b))
        yield

    def head_lag_gen(l, s, t, rot, tiles, pool_chunks=(1, 4, 7), sq_pool=False):
        res = []
        yield from stats_lag_gen(t, res, rot, tiles, 3, sq_pool)
        yield from apply_gen(l, s, t, res[0], "H", pool_chunks)

    WBUF = {"A": (SAv, ("SA",)), "B": (SBv, ("SB",)), "C": (SCv, ("SC",))}
    WPLAN = {0: "A", 1: "BC", 2: "BAB", 3: "ACAC"}

    def pm_gen(l, s, t):
        o = (l * 3 + s) * 8
        psl = piece_slot[("pool",)]
        if t == 0:
            add("dve", I("memset", HFv[:, :, 0:16], 0.0), reads=[INDONE] + ALLH, writes=[("HFh",)])
        else:
            add("act", I("activation", out=HFv[:, :, 0:16], in_=HALO[:], func=AF.Copy), reads=["HALO", INDONE] + ALLH, writes=[("HFh",)])
        yield
        res = []
        yield from stats_lag_gen(t, res, ps6, SQ_PM)
        rs = res[0]
        for c in range(NC):
            add("dve", I("tensor_tensor", out=HFv[:, c, 16:16 + TT], in0=X[:, c, tsl(t)], in1=TP[:, rs, 0:TT], op=ALU.mult),
                reads=[Xk(c, t), ("TP", rs), INDONE] + ALLH, writes=[("HF", c)])
            yield
        hf_all = [("HF", c) for c in range(NC)] + [("HFh",)]
        if t < NT - 1:
            add("act", I("activation", out=HALO[:], in_=HFv[:, :, 512:528], func=AF.Copy), reads=hf_all + ALLH, writes=["HALO"])
        for i in range(4):
            cs = slice(2 * i, 2 * i + 2)
            hk = [("HF", 2 * i), ("HF", 2 * i + 1), ("HFh",)]
            plan_i = WPLAN[i]
            cur, curk = WBUF[plan_i[0]]
            add(WENG, I("tensor_tensor", out=cur[:, :, 1:528], in0=HFv[:, cs, 1:528], in1=HFv[:, cs, 0:527], op=ALU.add),
                reads=hk + [INDONE] + ALLH, writes=[curk])
            sh = 2
            lo = 1
            for nb in plan_i[1:]:
                oth, othk = WBUF[nb]
                lo2 = lo + sh
                add(WENG, I("tensor_tensor", out=oth[:, :, lo2:528], in0=cur[:, :, lo2:528], in1=cur[:, :, lo2 - sh:528 - sh], op=ALU.add),
                    reads=[curk, INDONE] + ALLH, writes=[othk])
                cur, curk = oth, othk
                lo = lo2
                sh *= 2
            w = 2 ** (i + 1)
            yield
            PP = PPs
            ppk = ("PP",)
            add("dve", I("scalar_tensor_tensor", out=PP[:], in0=cur[:, :, 16:528], scalar=1.0 / w, in1=HFv[:, cs, 16:528],
                         op0=ALU.mult, op1=ALU.subtract),
                reads=[curk, INDONE] + hk + ALLH, writes=[ppk])
            if t == 0:
                for cc in range(2):
                    tmpi = tp_rot.next()
                    add("dve", I("tensor_tensor", out=TP[:, tmpi, 0:w - 1], in0=cur[:, cc, 16:16 + w - 1], in1=INVC[:, 0:w - 1], op=ALU.mult),
                        reads=[curk, "INVC"], writes=[("TP", tmpi)])
                    add("dve", I("tensor_tensor", out=PP[:, cc, 0:w - 1], in0=TP[:, tmpi, 0:w - 1],
                                 in1=HFv[:, 2 * i + cc, 16:16 + w - 1], op=ALU.subtract),
                        reads=[("TP", tmpi), ppk] + hk + ALLH, writes=[ppk])
            yield
            for oh in range(2):
                b = psA.next()
                dch = 2 * i + oh
                for kc in range(2):
                    add("pe", I("matmul", PS[b][:], SL[:, psl, (i * 2 + kc) * 256 + oh * 128:(i * 2 + kc) * 256 + (oh + 1) * 128],
                                PP[:, kc, :], start=(kc == 0), stop=(kc == 1)),
                        reads=[("SL", psl), ppk] + ALLH, writes=[("ps", b)])
                add("dve", I("scalar_tensor_tensor", out=X[:, dch, tsl(t)], in0=PS[b][:], scalar=GH[:, o + dch:o + dch + 1],
                             in1=X[:, dch, tsl(t)], op0=ALU.mult, op1=ALU.add),
                    reads=[("ps", b), ("GH", l, s), Xk(dch, t)], writes=[Xk(dch, t)])
            yield

    RL = role_of(1, 2)
    Ov = [RL.hbuf[:, 4 * i:4 * i + 4, :].bitcast(F32).rearrange("p a b -> p (a b)").rearrange("p (c k) -> p c k", k=TT) for i in range(2)]
    O_KEYS = [[(RL.hn, c, t) for c in range(4 * i, 4 * i + 4) for t in range(NT)] for i in range(2)]

    def final_gen(t):
        res = []
        yield from stats_gen(t, res)
        yield from final_apply_gen(t, res[0])

    def final_apply_gen(t, rs):
        ob = t % 2
        for c in range(NC):
            add("dve", I("scalar_tensor_tensor", out=Ov[ob][:, c, :], in0=X[:, c, tsl(t)], scalar=SM[:, SM_FG + c:SM_FG + c + 1],
                         in1=TP[:, rs, 0:TT], op0=ALU.mult, op1=ALU.mult),
                reads=[Xk(c, t), "SM", ("TP", rs)], writes=[("O", ob, c)] + O_KEYS[ob])
            yield
        add("sp", I("dma_start", out=outT_v[:, :, tsl(t)], in_=Ov[ob][:]), reads=[("O", ob, c) for c in range(NC)] + O_KEYS[ob], dma=("OUT", ob))

    def debug_store():
        for t in range(NT):
            add("sp", I("dma_start", out=outT_v[:, :, tsl(t)], in_=X[:, :, tsl(t)]),
                reads=[Xk(c, t) for c in range(NC)], dma=("OUT", t))
        add("sp", None, writes=[Xk(c, t) for c in range(NC) for t in range(NT)])

    add("sp", I("dma_start", out=SM[:], in_=small_d), writes=["SM"], dma="SM")
    add("sp", I("dma_start", out=CB[:], in_=cb_d), writes=["CB"], dma="CB")
    add("act", I("activation", out=CB[:], in_=CB[:], func=AF.Silu), reads=["CB"], writes=["CB"])
    add("dve", I("memset", ONES[:], 1.0), writes=["ONES"])
    add("dve", I("memset", EPSB[:], EPS), writes=["EPSB"])
    add("dve", I("memset", MHALF[:], -0.5), writes=["MHALF"])
    if n_sub_run >= 5:
        pm_consts()
    add("sp", I("dma_start", out=X[:, :, tsl(0)], in_=xT_v[:, :, tsl(0)]), writes=[Xk(c, 0) for c in range(NC)], dma=("X", 0))
    ffn_prefetch(0, 0, 1)
    for q in range(16):
        mod_staged[q] = mod_dma(q)
    for t in range(1, NT):
        add("sp", I("dma_start", out=X[:, :, tsl(t)], in_=xT_v[:, :, tsl(t)]), writes=[Xk(c, t) for c in range(NC)], dma=("X", t))
    mod_finalize_norm(0, 0)

    seq = [(0, 0), (0, 1), (0, 2), (1, 0), (1, 1), (1, 2)][:n_sub_run]
    last = len(seq) - 1

    def body_of(idx):
        l, s = seq[idx]
        if s == 0:
            ffn_body(l, s, 0, 72 if l == 0 else 144, wave_cb=startup_cb if idx == 0 else None)
        elif s == 2:
            ffn_body(l, s, 1, 144)
        else:
            ab_body(l, s)

    def tail_of(idx):
        l, s = seq[idx]
        if s == 1:
            return ab_tail(l, s)
        return ffn_tail(l, s)

    def prefetch_of(idx):
        l, s = seq[idx]
        mod_finalize_norm(l, s)
        if s == 0:
            ffn_prefetch(l, 0)
        elif s == 2:
            ffn_prefetch(l, 1, 2 if l == 1 else 3)
        elif l == 0:
            ab_prefetch()
        else:
            pm_prefetch()
        if s == 1:
            mod_finalize_gate(l, s)

    st_rs = {}

    def st_stats(t):
        res = []
        drain(stats_gen(t, res, None, True))
        st_rs[t] = res[0]

    def st_apply(t):
        drain(apply_gen(0, 0, t, st_rs[t], "H"))

    def startup_cb(t):
        if t + 1 < NT:
            st_apply(t + 1)
        if t + 2 < NT:
            st_stats(t + 2)

    st_stats(0)
    for j in GROUPS[0][1:3]:
        load_piece(("win", 0, 0, j), None, [("MODT", 11)])
    st_apply(0)
    st_stats(1)

    idx = 0
    while idx <= last:
        l, s = seq[idx]
        if (l, s) == (1, 1):
            idx += 1
            continue
        body_of(idx)
        stages = [tail_of(idx)]
        weights = [1]
        nxt = idx + 1
        if nxt <= last:
            ln, sn = seq[nxt]
            prefetch_of(nxt)
            if (ln, sn) == (1, 1):
                pm_load()
                stages.append(lambda t: pm_gen(1, 1, t))
                weights.append(3)
                if nxt + 1 <= last:
                    prefetch_of(nxt + 1)
                    ln2, sn2 = seq[nxt + 1]
                    stages.append(lambda t, ln2=ln2, sn2=sn2: head_lag_gen(ln2, sn2, t, ps7, SQ_HD, (), True))
                    weights.append(2)
            else:
                boundary(stages[0], lambda t, rs, ln=ln, sn=sn: apply_gen(ln, sn, t, rs, "H"))
                idx += 1
                continue
        elif run_final and n_sub_run == 6:
            boundary(stages[0], final_apply_gen)
            idx += 1
            continue
        if len(stages) >= 2:
            pm_stage2(stages[0], stages[2] if len(stages) == 3 else None)
        else:
            run_stages(stages, weights)
        idx += 1
    if run_final and n_sub_run == 6:
        add("sp", None, writes=O_KEYS[0] + O_KEYS[1] + [("O", ob, c) for ob in range(2) for c in range(NC)])
    else:
        debug_store()

    with nc.Block() as block:
        Sd.emit(nc, block)
    nc._sched = Sd
    return nc


_CACHE = {}


def kernel(x, c, norm_g, w_mod, b_mod, w_ffn_in, w_ffn_out, ab_w_in, ab_norm_v, ab_w_s, ab_b_s,
           ab_conv_w, ab_w_out, pool_w_grp, pool_scale, final_g, _n_sub_run=N_SUB_RUN, _run_final=RUN_FINAL):
    inp = dict(x=x, c=c, norm_g=norm_g, w_mod=w_mod, b_mod=b_mod, w_ffn_in=w_ffn_in, w_ffn_out=w_ffn_out,
               ab_w_in=ab_w_in, ab_norm_v=ab_norm_v, ab_w_s=ab_w_s, ab_b_s=ab_b_s, ab_conv_w=ab_conv_w,
               ab_w_out=ab_w_out, pool_w_grp=pool_w_grp, pool_scale=pool_scale, final_g=final_g)
    inp = {k: np.asarray(v, dtype=np.float32) for k, v in inp.items()}
    key = (_n_sub_run, _run_final)
    if key not in _CACHE:
        _CACHE[key] = build_nc(_n_sub_run, _run_final)
    nc = _CACHE[key]
    wp = build_pieces(inp)
    small = build_small(inp)
    wmod = build_wmod(inp)
    wst = np.ascontiguousarray(inp["ab_w_s"][0].transpose(2, 0, 1).reshape(128, 512))
    in_maps = []
    for b in range(8):
        in_maps.append({
            "xT": np.ascontiguousarray(inp["x"][b].T),
            "cb": np.ascontiguousarray(np.broadcast_to(inp["c"][b][None, :], (128, D))),
            "small": small, "wst": wst, "wmod": wmod, "wp": wp,
        })
    res = run_bass_kernel_spmd(nc, in_maps, core_ids=list(range(8)))
    out = np.stack([np.ascontiguousarray(r["outT"].T) for r in res.results], axis=0)
    return out.astype(np.float32)
```

```python
import numpy as np
import concourse.bass as bass
import concourse.mybir as mybir
from concourse.bass_utils import run_bass_kernel_spmd

F32 = mybir.dt.float32
BF16 = mybir.dt.bfloat16
AF = mybir.ActivationFunctionType
ALU = mybir.AluOpType

D = 1024
S = 2048
NC = 8
NT = 4
TT = 512
D_FF = 2816
NJ = 22
GROUPS = [list(range(0, 8)), list(range(8, 15)), list(range(15, 22))]
EPS = 1e-6
N_SUB_RUN = 6
RUN_FINAL = True


class Op:
    __slots__ = ("idx", "eng", "fn", "cdeps", "ddeps", "semkey", "pos", "seq",
                 "signal", "dma_target")


class Sched:
    ENGS = ("pe", "act", "dve", "pool", "sp")
    WINDOW = 10 ** 9

    def __init__(self):
        self.ops = []
        self.lastw = {}
        self.rd_eng = {}
        self.rd_dma = {}
        self.npos = {e: 0 for e in self.ENGS}

    def add(self, eng, fn, reads=(), writes=(), dma=None):
        ops = self.ops
        idx = len(ops)
        deps = set()
        for k in reads:
            w = self.lastw.get(k)
            if w is not None:
                deps.add(w)
        for k in writes:
            w = self.lastw.get(k)
            if w is not None:
                deps.add(w)
            r = self.rd_eng.get(k)
            if r:
                deps.update(r.values())
            r = self.rd_dma.get(k)
            if r:
                deps.update(r)
        op = Op()
        op.idx = idx
        op.eng = eng
        op.fn = fn
        op.semkey = dma
        cd = {}
        dd = []
        for d in deps:
            o = ops[d]
            if o.semkey is not None:
                dd.append(d)
            elif cd.get(o.eng, -1) < d:
                cd[o.eng] = d
        op.cdeps = cd
        op.ddeps = dd
        op.pos = self.npos[eng]
        self.npos[eng] += 1
        op.signal = False
        op.seq = 0
        op.dma_target = 0
        ops.append(op)
        for k in reads:
            if dma is not None:
                self.rd_dma.setdefault(k, []).append(idx)
            else:
                self.rd_eng.setdefault(k, {})[eng] = idx
        for k in writes:
            self.lastw[k] = idx
            self.rd_eng[k] = {}
            self.rd_dma[k] = []
        return idx

    def _skip(self, op, o):
        if o.eng == op.eng and op.semkey is None:
            if o.eng == "pe":
                return True
            if op.pos - o.pos > self.WINDOW:
                return True
        return False

    def emit(self, nc, block):
        ops = self.ops
        for op in ops:
            for e, d in op.cdeps.items():
                o = ops[d]
                if not self._skip(op, o):
                    o.signal = True
        cnt = {e: 0 for e in self.ENGS}
        dcnt = {}
        for op in ops:
            if op.semkey is not None:
                dcnt[op.semkey] = dcnt.get(op.semkey, 0) + 16
                op.dma_target = dcnt[op.semkey]
            elif op.signal:
                cnt[op.eng] += 1
                op.seq = cnt[op.eng]
        self.sig_counts = cnt
        esem = {e: nc.alloc_semaphore("s_" + e) for e in self.ENGS}
        dsem = {k: nc.alloc_semaphore("d_%d" % i) for i, k in enumerate(dcnt)}
        self.n_sems = len(esem) + len(dsem)
        by_eng = {e: [] for e in self.ENGS}
        for op in ops:
            by_eng[op.eng].append(op)

        def run(eng_name, E):
            waited = {}
            for op in by_eng[eng_name]:
                for e, d in op.cdeps.items():
                    o = ops[d]
                    if not o.signal or self._skip(op, o):
                        continue
                    if waited.get(("e", e), 0) < o.seq:
                        E.wait_ge(esem[e], o.seq)
                        waited[("e", e)] = o.seq
                for d in op.ddeps:
                    o = ops[d]
                    if waited.get(("d", o.semkey), 0) < o.dma_target:
                        E.wait_ge(dsem[o.semkey], o.dma_target)
                        waited[("d", o.semkey)] = o.dma_target
                if op.fn is None:
                    continue
                ins = op.fn(E)
                if op.semkey is not None:
                    ins.then_inc(dsem[op.semkey], 16)
                elif op.signal:
                    ins.then_inc(esem[op.eng], 1)

        if by_eng["sp"]:
            block.sync(lambda E: run("sp", E))
        if by_eng["pool"]:
            block.gpsimd(lambda E: run("pool", E))
        if by_eng["act"]:
            block.scalar(lambda E: run("act", E))
        if by_eng["dve"]:
            block.vector(lambda E: run("dve", E))
        if by_eng["pe"]:
            block.tensor(lambda E: run("pe", E))


SM_BMOD = 0
SM_G = 144
SM_FG = 192
SM_NV = 200
SM_CW = 204
SM_PS = 216
SM_BS = 224
SM_N = 736


def _blk(w, cols):
    return np.ascontiguousarray(w[:, cols].reshape(8, 128, -1).transpose(1, 0, 2))


def piece_plan():
    plan = []
    for l in range(2):
        for i in range(2):
            if i == 1:
                if l == 0:
                    plan += [("abv", 0), ("abv", 1), ("abu", 0), ("abu", 1)]
                    for hf in range(2):
                        plan += [("abb", 2, hf), ("abb", 3, hf), ("abb", 4, hf)]
                    plan += [("abo", dp) for dp in range(4)]
                else:
                    plan += [("pool",)]
            for g, ks in enumerate(GROUPS):
                for j in ks[:4]:
                    plan.append(("win", l, i, j))
                for dp in range(4):
                    plan.append(("wout", l, i, g, dp))
                for j in ks[4:]:
                    plan.append(("win", l, i, j))
    return plan


def build_pieces(inp):
    plan = piece_plan()
    out = np.zeros((len(plan), 128, 2048), np.float32)
    w_in = inp["w_ffn_in"]
    w_out = inp["w_ffn_out"]
    abw = inp["ab_w_in"][0]
    abo = inp["ab_w_out"][0]
    pw = inp["pool_w_grp"][0]
    for n, p in enumerate(plan):
        k = p[0]
        if k == "win":
            _, l, i, j = p
            w = w_in[l, i]
            a = _blk(w, slice(j * 128, (j + 1) * 128))
            b = _blk(w, slice(D_FF + j * 128, D_FF + (j + 1) * 128))
            out[n] = np.concatenate([a, b], axis=2).reshape(128, 2048)
        elif k == "wout":
            _, l, i, g, dp = p
            w = w_out[l, i]
            ks = GROUPS[g]
            blk = np.zeros((128, 2, 8, 128), np.float32)
            for dd in range(2):
                for kk, kc in enumerate(ks):
                    blk[:, dd, kk, :] = w[kc * 128:(kc + 1) * 128, (2 * dp + dd) * 128:(2 * dp + dd + 1) * 128]
            out[n] = blk.reshape(128, 2048)
        elif k == "abv":
            hp = p[1]
            out[n] = _blk(abw, slice(512 + hp * 256, 512 + (hp + 1) * 256)).reshape(128, 2048)
        elif k == "abu":
            hp = p[1]
            blk = np.stack([_blk(abw, slice((2 * hp + ff) * 128, (2 * hp + ff + 1) * 128)) for ff in range(2)], axis=1)
            out[n] = blk.reshape(128, 2048)
        elif k == "abb":
            _, sec, hf = p
            base = sec * 512
            blk = np.stack([_blk(abw, slice(base + (2 * hf + ff) * 128, base + (2 * hf + ff + 1) * 128)) for ff in range(2)], axis=1)
            out[n] = blk.reshape(128, 2048)
        elif k == "abo":
            dp = p[1]
            blk = np.stack([_blk(abo, slice((2 * dp + dd) * 128, (2 * dp + dd + 1) * 128)) for dd in range(2)], axis=1)
            out[n] = blk.reshape(128, 2048)
        elif k == "pool":
            blk = np.zeros((128, 4, 2, 256), np.float32)
            for i in range(4):
                for kc in range(2):
                    blk[:, i, kc, :] = pw[i, kc * 128:(kc + 1) * 128, :]
            out[n] = blk.reshape(128, 2048)
    return out


def build_small(inp):
    sm = np.zeros((128, SM_N), np.float32)
    sm[:, SM_BMOD:SM_BMOD + 144] = inp["b_mod"].reshape(2 * 72, 128).T
    sm[:, SM_G:SM_G + 48] = inp["norm_g"].reshape(48, 128).T
    sm[:, SM_FG:SM_FG + 8] = inp["final_g"].reshape(8, 128).T
    sm[:, SM_NV:SM_NV + 4] = inp["ab_norm_v"][0].reshape(4, 128).T
    sm[:, SM_CW:SM_CW + 12] = inp["ab_conv_w"][0].reshape(3, 4, 128).transpose(2, 0, 1).reshape(128, 12)
    sm[:, SM_PS:SM_PS + 8] = inp["pool_scale"][0].reshape(8, 128).T
    sm[:, SM_BS:SM_BS + 512] = np.broadcast_to(inp["ab_b_s"][0].reshape(1, 512), (128, 512))
    return sm


def build_wmod(inp):
    wm = inp["w_mod"]
    return np.ascontiguousarray(wm.transpose(0, 2, 1).reshape(2 * 72, 128, 1024))


def I(name, *a, **k):
    return lambda E: getattr(E, name)(*a, **k)


def build_nc(n_sub_run=N_SUB_RUN, run_final=RUN_FINAL):
    nc = bass.Bass("TRN2", target_bir_lowering=False)
    plan = piece_plan()
    xT = nc.dram_tensor("xT", [D, S], F32, kind="ExternalInput").ap()
    cb_d = nc.dram_tensor("cb", [128, D], F32, kind="ExternalInput").ap()
    small_d = nc.dram_tensor("small", [128, SM_N], F32, kind="ExternalInput").ap()
    wst_d = nc.dram_tensor("wst", [128, 512], F32, kind="ExternalInput").ap()
    wmod_d = nc.dram_tensor("wmod", [144, 128, D], F32, kind="ExternalInput").ap()
    wp_d = nc.dram_tensor("wp", [len(plan), 128, 2048], F32, kind="ExternalInput").ap()
    outT = nc.dram_tensor("outT", [D, S], F32, kind="ExternalOutput").ap()
    xT_v = xT.rearrange("(c p) t -> p c t", p=128)
    outT_v = outT.rearrange("(c p) t -> p c t", p=128)

    sb = nc.alloc_sbuf_tensor
    X = sb("X", [128, NC, S], F32)
    H = sb("H", [128, NC, S], BF16)
    AB = sb("ACTB", [128, 8, S], BF16)
    NSLOT = 7
    SL = sb("SLOTS", [128, NSLOT, 2048], BF16)
    NWM = 3
    WM = sb("WMOD", [128, NWM, D], F32)
    NTP = 8
    TPW = 516
    TP = sb("TP", [128, NTP, TPW], F32)
    NSQ = 3
    SQ = sb("SQ", [128, NSQ, TT], BF16)
    CB = sb("CB", [128, D], F32)
    JUNK = sb("JUNK", [128, D], BF16)
    SM = sb("SM", [128, SM_N], F32)
    MODT = sb("MODT", [128, 144], F32)
    AS = sb("AS", [128, 48], F32)
    GH = sb("GH", [128, 48], F32)
    WSM = sb("WSM", [128, 4, 128], BF16)
    ONES = sb("ONES", [128, 128], BF16)
    VH = sb("VHAT", [128, 4, 512], BF16)
    HALO = sb("HALO", [128, 8, 16], F32)
    INVC = sb("INVC", [128, 16], F32)
    STAT = sb("STAT", [128, 4, 16], F32)
    PMC = sb("PMC", [128, 12, 128], BF16)
    GS = sb("GS", [128, 8], F32)
    CORR = sb("CORR", [128, 4, 16], F32)
    EPSB = sb("EPSB", [128, 1], F32)
    MHALF = sb("MHALF", [128, 1], F32)
    PS = [nc.alloc_psum_tensor("ps%d" % b, [128, TT], F32) for b in range(8)]

    Sd = Sched()
    add = Sd.add

    class Rot:
        def __init__(self, n, base=0):
            self.n = n
            self.i = 0
            self.base = base

        def next(self):
            v = self.base + self.i % self.n
            self.i += 1
            return v

    rs_rot = Rot(3, 0)
    tp_rot = Rot(5, 3)
    sq_rot = Rot(NSQ)
    psA = Rot(6)
    psB = Rot(2, 6)
    ps6 = Rot(1, 6)
    ps7 = Rot(1, 7)
    win_rot = Rot(3)
    wm_rot = Rot(NWM)
    stat_rot = Rot(4)

    class Role:
        def __init__(self, hbuf, hn, abuf, an):
            self.hbuf, self.hn, self.abuf, self.an = hbuf, hn, abuf, an

    R0 = Role(H, "H", AB, "A")
    R1 = Role(AB, "A", H, "H")

    def role_of(l, s):
        return R1 if (l, s) == (1, 2) else R0

    def Hk(c, t):
        return ("H", c, t)

    def Xk(c, t):
        return ("X", c, t)

    def tsl(t):
        return slice(t * TT, (t + 1) * TT)

    piece_idx = {p: n for n, p in enumerate(plan)}
    piece_slot = {}

    def load_piece(p, slot=None, after=()):
        if p in piece_slot:
            return piece_slot[p]
        if slot is None:
            slot = win_rot.next()
        n = piece_idx[p]
        piece_slot[p] = slot
        add("pool", I("dma_start", out=SL[:, slot, :], in_=wp_d[n], max_dma_last_dim=8192),
            reads=list(after), writes=[("SL", slot)], dma=("SL", slot))
        return slot

    mod_next = [0]

    ABf = AB[:].rearrange("p a b -> p (a b)")
    ABf32 = ABf.bitcast(F32)
    Hf32 = H[:].rearrange("p a b -> p (a b)").bitcast(F32)

    def mod_dma(q):
        if q < 8:
            k = q
            ap = ABf32[:, k * 1024:(k + 1) * 1024]
            keys = [("A", k, t) for t in range(NT)]
            add("sp", I("dma_start", out=ap, in_=wmod_d[q]), writes=keys, dma=("AM", k))
        elif q < 16:
            k = q - 8
            ap = Hf32[:, k * 1024:(k + 1) * 1024]
            keys = [("H", k, t) for t in range(NT)]
            add("sp", I("dma_start", out=ap, in_=wmod_d[q]), writes=keys, dma=("HM", k))
        else:
            s = wm_rot.next()
            ap = WM[:, s, :]
            keys = [("WM", s)]
            add("sp", I("dma_start", out=ap, in_=wmod_d[q]), writes=keys, dma=("WM", s))
        return ap, keys

    mod_staged = {}

    def mod_step():
        q = mod_next[0]
        if q >= 144:
            return
        mod_next[0] += 1
        if q in mod_staged:
            ap, keys = mod_staged.pop(q)
        else:
            ap, keys = mod_dma(q)
        add("dve", I("scalar_tensor_tensor", out=JUNK[:], in0=ap, scalar=1.0, in1=CB[:],
                     op0=ALU.mult, op1=ALU.mult, accum_out=MODT[:, q:q + 1]),
            reads=keys + ["CB"], writes=[("MODT", q), "JUNK"])

    def mod_until(q_end):
        while mod_next[0] < q_end:
            mod_step()

    def mod_finalize_norm(l, s):
        base = l * 72 + s * 24
        mod_until(base + 16)
        o = (l * 3 + s) * 8
        add("dve", I("tensor_tensor", out=MODT[:, base:base + 16], in0=MODT[:, base:base + 16],
                     in1=SM[:, SM_BMOD + base:SM_BMOD + base + 16], op=ALU.add),
            reads=[("MODT", base + i) for i in range(16)] + ["SM"], writes=[("MODF", l, s)])
        add("dve", I("scalar_tensor_tensor", out=AS[:, o:o + 8], in0=MODT[:, base + 8:base + 16], scalar=1.0,
                     in1=SM[:, SM_G + o:SM_G + o + 8], op0=ALU.add, op1=ALU.mult),
            reads=[("MODF", l, s), "SM"], writes=[("AS", l, s)])

    def mod_finalize_gate(l, s):
        base = l * 72 + s * 24
        mod_until(base + 24)
        o = (l * 3 + s) * 8
        gsl = slice(base + 16, base + 24)
        add("dve", I("tensor_tensor", out=MODT[:, gsl], in0=MODT[:, gsl],
                     in1=SM[:, SM_BMOD + base + 16:SM_BMOD + base + 24], op=ALU.add),
            reads=[("MODT", base + 16 + i) for i in range(8)] + ["SM"], writes=[("MODG", l, s)])
        if s != 1:
            add("dve", I("tensor_scalar", out=GH[:, o:o + 8], in0=MODT[:, gsl], scalar1=0.5, scalar2=None, op0=ALU.mult),
                reads=[("MODG", l, s)], writes=[("GH", l, s)])
        elif l == 0:
            add("dve", I("tensor_copy", GH[:, o:o + 8], MODT[:, gsl]), reads=[("MODG", l, s)], writes=[("GH", l, s)])
        else:
            add("dve", I("tensor_tensor", out=GS[:], in0=MODT[:, gsl], in1=SM[:, SM_PS:SM_PS + 8], op=ALU.mult),
                reads=[("MODG", l, s), "SM"], writes=["GS"])
            for i in range(4):
                add("dve", I("tensor_scalar", out=GH[:, o + 2 * i:o + 2 * i + 2], in0=GS[:, 2 * i:2 * i + 2],
                             scalar1=1.0 / (2 ** (i + 1)), scalar2=None, op0=ALU.mult),
                    reads=["GS"], writes=[("GH", l, s)] if i == 3 else [("GHpart", i)])

    def stepper(gl):
        gens = [[iter(g), w, True] for g, w in gl]
        while any(a for _, _, a in gens):
            for ent in gens:
                if not ent[2]:
                    continue
                for _ in range(ent[1]):
                    try:
                        next(ent[0])
                    except StopIteration:
                        ent[2] = False
                        break

    def run_stages(stages, weights, drop_last=False):
        nst = len(stages)
        for step in range(NT + nst - 1 - (1 if drop_last else 0)):
            gl = []
            for si, st in enumerate(stages):
                t = step - si
                if 0 <= t < NT:
                    gl.append((st(t), weights[si]))
            stepper(gl)

    def drain(g):
        for _ in g:
            pass

    def stats_gen(t, res, rot=None, sq_pool=False):
        b = (rot or psB).next()
        for c in range(NC):
            q = sq_rot.next()
            if sq_pool:
                add("pool", I("tensor_tensor", out=SQ[:, q, :], in0=X[:, c, tsl(t)], in1=X[:, c, tsl(t)], op=ALU.mult),
                    reads=[Xk(c, t)], writes=[("SQ", q)])
            else:
                add("act", I("activation", out=SQ[:, q, :], in_=X[:, c, tsl(t)], func=AF.Square),
                    reads=[Xk(c, t)], writes=[("SQ", q)])
            add("pe", I("matmul", PS[b][:], ONES[:], SQ[:, q, :], start=(c == 0), stop=(c == NC - 1)),
                reads=[("SQ", q), "ONES"], writes=[("ps", b)])
            yield
        res.append(rstd_from(b))
        yield

    def rstd_from(b):
        sd = tp_rot.next()
        add("act", I("activation", out=TP[:, sd, 0:TT], in_=PS[b][:], func=AF.Ln, scale=1.0 / D, bias=EPSB[:, 0:1]),
            reads=[("ps", b), "EPSB"], writes=[("TP", sd)])
        rs = rs_rot.next()
        add("act", I("activation", out=TP[:, rs, 0:TT], in_=TP[:, sd, 0:TT], func=AF.Exp, scale=-0.5),
            reads=[("TP", sd)], writes=[("TP", rs)])
        return rs

    SQ8 = [(SQ[:, 0, :], ("SQ", 0)), (SQ[:, 1, :], ("SQ", 1)), (SQ[:, 2, :], ("SQ", 2)),
           (VH[:, 0, :], ("VH", 0)), (VH[:, 1, :], ("VH", 1)), (VH[:, 2, :], ("VH", 2)), (VH[:, 3, :], ("VH", 3)),
           (JUNK[:, 0:TT], "JUNK")]

    def square1(t, c):
        ap, k = SQ8[c]
        add("act", I("activation", out=ap, in_=X[:, c, tsl(t)], func=AF.Square), reads=[Xk(c, t)], writes=[k])

    def squares8(t):
        for c in range(NC):
            square1(t, c)

    def tail_with_squares(tg, t_sq):
        c = 2
        for _ in tg:
            square1(t_sq, c)
            c += 1
            yield

    def ones8(t):
        b = psB.next()
        for c in range(NC):
            ap, k = SQ8[c]
            add("pe", I("matmul", PS[b][:], ONES[:], ap, start=(c == 0), stop=(c == NC - 1)), reads=[k, "ONES"], writes=[("ps", b)])
        return rstd_from(b)

    def boundary(tail_fn, head_apply):
        pre_sq = head_apply is not None
        for step in range(NT + (1 if head_apply else 0)):
            tt = step if step < NT else None
            th = step - 1 if (head_apply and step >= 1) else None
            tg = tail_fn(tt) if tt is not None else None
            if th is not None and not (pre_sq and tg is None):
                squares8(th)
            if tg is not None:
                for _ in range(2):
                    next(tg)
            ag = None
            if th is not None:
                rs = ones8(th)
                ag = head_apply(th, rs)
            gl = []
            if tg is not None:
                if pre_sq and tt == NT - 1:
                    for c in range(2):
                        square1(tt, c)
                    gl.append((tail_with_squares(tg, tt), 1))
                else:
                    gl.append((tg, 1))
            if ag is not None:
                gl.append((ag, 2))
            stepper(gl)

    def apply_gen(l, s, t, rs, kind, pool_chunks=()):
        role = role_of(l, s)
        o = (l * 3 + s) * 8
        sh = l * 72 + s * 24
        for c in range(NC):
            tm = tp_rot.next()
            add("pool" if c in pool_chunks else "dve",
                I("tensor_tensor", out=TP[:, tm, 0:TT], in0=X[:, c, tsl(t)], in1=TP[:, rs, 0:TT], op=ALU.mult),
                reads=[Xk(c, t), ("TP", rs)], writes=[("TP", tm)])
            if kind == "H":
                add("act", I("activation", out=role.hbuf[:, c, tsl(t)], in_=TP[:, tm, 0:TT], func=AF.Identity,
                             scale=AS[:, o + c:o + c + 1], bias=MODT[:, sh + c:sh + c + 1]),
                    reads=[("TP", tm), ("AS", l, s), ("MODF", l, s)], writes=[(role.hn, c, t)])
            else:
                add("act", I("activation", out=HFv[:, c, 16:16 + TT], in_=TP[:, tm, 0:TT], func=AF.Identity,
                             scale=AS[:, o + c:o + c + 1], bias=MODT[:, sh + c:sh + c + 1]),
                    reads=[("TP", tm), ("AS", l, s), ("MODF", l, s), INDONE] + ALLH, writes=[("HF", c)])
            yield

    def head_gen(l, s, t, kind="H", rot=None):
        res = []
        yield from stats_gen(t, res, rot)
        yield from apply_gen(l, s, t, res[0], kind)

    TD = ("TAILDONE",)

    INDONE = ("INDONE",)

    def outproj_gen(l, s, t, slot_of, nk, rot, mark=False):
        o = (l * 3 + s) * 8
        role = role_of(l, s)
        for dch in range(NC):
            b = rot.next()
            slot = slot_of(dch // 2)
            dd = dch % 2
            for kk in range(nk):
                wr = [("ps", b)]
                if mark and t == NT - 1 and dch == NC - 1 and kk == nk - 1:
                    wr.append(TD)
                add("pe", I("matmul", PS[b][:], SL[:, slot, (dd * 8 + kk) * 128:(dd * 8 + kk + 1) * 128], role.abuf[:, kk, tsl(t)],
                            start=(kk == 0), stop=(kk == nk - 1)),
                    reads=[("SL", slot), (role.an, kk, t)], writes=wr)
            add("dve", I("scalar_tensor_tensor", out=X[:, dch, tsl(t)], in0=PS[b][:], scalar=GH[:, o + dch:o + dch + 1],
                         in1=X[:, dch, tsl(t)], op0=ALU.mult, op1=ALU.add),
                reads=[("ps", b), ("GH", l, s), Xk(dch, t)], writes=[Xk(dch, t)])
            yield

    def ffn_inproj(l, i, j, kk, t, slot, mark=False):
        role = role_of(l, 2 * i)
        bg = psA.next()
        bu = psA.next()
        for c in range(NC):
            add("pe", I("matmul", PS[bg][:], SL[:, slot, c * 256:c * 256 + 128], role.hbuf[:, c, tsl(t)], start=(c == 0), stop=(c == NC - 1)),
                reads=[("SL", slot), (role.hn, c, t)], writes=[("ps", bg)])
        for c in range(NC):
            wr = [("ps", bu)]
            if mark and c == NC - 1:
                wr.append(INDONE)
            add("pe", I("matmul", PS[bu][:], SL[:, slot, c * 256 + 128:c * 256 + 256], role.hbuf[:, c, tsl(t)], start=(c == 0), stop=(c == NC - 1)),
                reads=[("SL", slot), (role.hn, c, t)], writes=wr)
        sg = tp_rot.next()
        add("act", I("activation", out=TP[:, sg, 0:TT], in_=PS[bg][:], func=AF.Silu), reads=[("ps", bg)], writes=[("TP", sg)])
        add("dve", I("tensor_tensor", out=role.abuf[:, kk, tsl(t)], in0=TP[:, sg, 0:TT], in1=PS[bu][:], op=ALU.mult),
            reads=[("TP", sg), ("ps", bu)], writes=[(role.an, kk, t)])

    inproj_done = set()

    def early_wave_gen(l, i, npieces, ntiles):
        ks = GROUPS[0]
        for kk in range(npieces):
            slot = load_piece(("win", l, i, ks[kk]))
            for t in range(ntiles):
                ffn_inproj(l, i, ks[kk], kk, t, slot)
                inproj_done.add((l, i, ks[kk], t))
                yield

    def ffn_prefetch(l, i, n=3):
        for j in GROUPS[0][:n]:
            load_piece(("win", l, i, j))

    def ffn_body(l, s, i, mod_bg_until, wave_cb=None):
        for g, ks in enumerate(GROUPS):
            nk = len(ks)
            kk0 = 0
            if g == 0 and wave_cb is not None:
                slots = [load_piece(("win", l, i, j)) for j in ks[:3]]
                for t in range(NT):
                    for kk in range(3):
                        ffn_inproj(l, i, ks[kk], kk, t, slots[kk])
                        if mod_next[0] < mod_bg_until:
                            mod_step()
                    wave_cb(t)
                kk0 = 3
            for kk in range(kk0, nk):
                j = ks[kk]
                if kk == 4:
                    for dp in range(4):
                        load_piece(("wout", l, i, g, dp), 3 + dp)
                slot = load_piece(("win", l, i, j))
                for t in range(NT):
                    if (l, i, j, t) in inproj_done:
                        continue
                    ffn_inproj(l, i, j, kk, t, slot, mark=((l, s) == (1, 0) and g == 2 and kk == nk - 1 and t == NT - 1))
                    if mod_next[0] < mod_bg_until:
                        mod_step()
            if g == 0:
                mod_finalize_gate(l, s)

            if g < 2:
                for t in range(NT):
                    drain(outproj_gen(l, s, t, lambda dp: 3 + dp, nk, psB))

    def ffn_tail(l, s):
        return lambda t: outproj_gen(l, s, t, lambda dp: 3 + dp, len(GROUPS[2]), psA, mark=(l == 1 and s == 0))

    AB_OSLOT = {0: 3, 1: 4, 2: 5, 3: 6}

    def ab_prefetch():
        load_piece(("abv", 0), win_rot.next())
        load_piece(("abv", 1), win_rot.next())
        load_piece(("abu", 0), win_rot.next())

    def ab_body(l, s):
        wt = tp_rot.next()
        add("sp", I("dma_start", out=TP[:, wt, 0:512], in_=wst_d), writes=[("TP", wt)], dma="WST")
        add("pool", I("affine_select", out=TP[:, wt, 0:512].rearrange("p (h t) -> p h t", h=4),
                      in_=TP[:, wt, 0:512].rearrange("p (h t) -> p h t", h=4),
                      pattern=[[0, 4], [1, 128]], compare_op=ALU.is_ge, fill=0.0, base=0, channel_multiplier=-1),
            reads=[("TP", wt)], writes=[("TP", wt)])
        add("dve", I("tensor_copy", WSM[:].rearrange("p h t -> p (h t)"), TP[:, wt, 0:512]), reads=[("TP", wt)], writes=["WSM"])
        vs = [piece_slot[("abv", 0)], piece_slot[("abv", 1)]]
        us = [piece_slot[("abu", 0)], load_piece(("abu", 1), 3)]
        bsl = {0: (4, 5, 6)}
        for si, sec in enumerate((2, 3, 4)):
            load_piece(("abb", sec, 0), bsl[0][si])
        for t in range(NT):
            pend = {}

            def v_stats(n):
                tok = slice(t * TT + n * 128, t * TT + (n + 1) * 128)
                b = psA.next()
                for hp in range(2):
                    for c in range(NC):
                        add("pe", I("matmul", PS[b][:, hp * 256:(hp + 1) * 256], H[:, c, tok], SL[:, vs[hp], c * 256:(c + 1) * 256],
                                    start=(c == 0), stop=(c == NC - 1)),
                            reads=[("SL", vs[hp]), Hk(c, t)], writes=[("ps", b)])
                gv = tp_rot.next()
                add("act", I("activation", out=TP[:, gv, 0:TT], in_=PS[b][:], func=AF.Gelu_apprx_tanh),
                    reads=[("ps", b)], writes=[("TP", gv)])
                st = stat_rot.next()
                add("dve", I("bn_stats", out=STAT[:, st, 0:6], in_=TP[:, gv, 0:TT]), reads=[("TP", gv)], writes=[("ST", st, 0)])
                add("dve", I("bn_aggr", out=STAT[:, st, 6:8], in_=STAT[:, st, 0:6]), reads=[("ST", st, 0)], writes=[("ST", st, 1)])
                add("pool", I("tensor_scalar", out=STAT[:, st, 8:9], in0=STAT[:, st, 7:8], scalar1=EPS, scalar2=None, op0=ALU.add),
                    reads=[("ST", st, 1)], writes=[("ST", st, 2)])
                add("pool", I("tensor_tensor", out=STAT[:, st, 9:10], in0=STAT[:, st, 8:9], in1=MHALF[:, 0:1], op=ALU.pow),
                    reads=[("ST", st, 2), "MHALF"], writes=[("ST", st, 3)])
                pend[n] = (gv, st)

            def v_norm(n):
                gv, st = pend[n]
                add("dve", I("tensor_scalar", out=VH[:, n, :], in0=TP[:, gv, 0:TT], scalar1=STAT[:, st, 6:7],
                             scalar2=STAT[:, st, 9:10], op0=ALU.subtract, op1=ALU.mult),
                    reads=[("TP", gv), ("ST", st, 1), ("ST", st, 3)], writes=[("VH", n)])

            v_stats(0)
            v_stats(1)
            v_norm(0)
            v_stats(2)
            v_norm(1)
            v_stats(3)
            v_norm(2)
            v_norm(3)

            gus = {}

            def u_part(hd, gu):
                bu = psA.next()
                usl = us[hd // 2]
                ff = hd % 2
                for c in range(NC):
                    add("pe", I("matmul", PS[bu][:], SL[:, usl, (ff * 8 + c) * 128:(ff * 8 + c + 1) * 128], H[:, c, tsl(t)],
                                start=(c == 0), stop=(c == NC - 1)),
                        reads=[("SL", usl), Hk(c, t)], writes=[("ps", bu)])
                add("act", I("activation", out=TP[:, gu, 0:TT], in_=PS[bu][:], func=AF.Gelu_apprx_tanh),
                    reads=[("ps", bu)], writes=[("TP", gu)])
                gus[hd] = gu

            def z_part(hd):
                gu = gus[hd]
                bz = psA.next()
                for n in range(4):
                    add("pe", I("matmul", PS[bz][:, n * 128:(n + 1) * 128], VH[:, n, hd * 128:(hd + 1) * 128], WSM[:, hd, :],
                                start=True, stop=True),
                        reads=[("VH", n), "WSM"], writes=[("ps", bz)])
                z = tp_rot.next()
                add("dve", I("scalar_tensor_tensor", out=TP[:, z, 0:TT].rearrange("p (n k) -> p n k", n=4),
                             in0=PS[bz][:].rearrange("p (n k) -> p n k", n=4), scalar=SM[:, SM_NV + hd:SM_NV + hd + 1],
                             in1=SM[:, SM_BS + hd * 128:SM_BS + (hd + 1) * 128].unsqueeze(1).broadcast_to([128, 4, 128]),
                             op0=ALU.mult, op1=ALU.add),
                    reads=[("ps", bz), "SM"], writes=[("TP", z)])
                add("dve", I("tensor_tensor", out=AB[:, hd, tsl(t)], in0=TP[:, z, 0:TT], in1=TP[:, gu, 0:TT], op=ALU.mult),
                    reads=[("TP", z), ("TP", gu)], writes=[("A", hd, t)])

            u_part(0, 0)
            u_part(1, 1)
            u_part(2, 2)
            u_part(3, tp_rot.next())
            z_part(0)
            z_part(1)
            z_part(2)
            z_part(3)
        bsl[1] = (vs[0], vs[1], us[0])
        for hf in range(2):
            sl3 = bsl[hf]
            if hf == 1:
                for si, sec in enumerate((2, 3, 4)):
                    load_piece(("abb", sec, 1), sl3[si])
                for dp in range(4):
                    load_piece(("abo", dp), AB_OSLOT[dp])
            for ff in range(2):
                q = 2 * hf + ff
                prevP = None
                for t in range(NT):
                    banks = [psA.next() for _ in range(3)]
                    for si in range(3):
                        for c in range(NC):
                            add("pe", I("matmul", PS[banks[si]][:], SL[:, sl3[si], (ff * 8 + c) * 128:(ff * 8 + c + 1) * 128], H[:, c, tsl(t)],
                                        start=(c == 0), stop=(c == NC - 1)),
                                reads=[("SL", sl3[si]), Hk(c, t)], writes=[("ps", banks[si])])
                    cgs = tp_rot.next()
                    add("act", I("activation", out=TP[:, cgs, 0:TT], in_=PS[banks[1]][:], func=AF.Copy),
                        reads=[("ps", banks[1])], writes=[("TP", cgs)])
                    P = tp_rot.next()
                    if prevP is None:
                        add("dve", I("memset", TP[:, P, 0:2], 0.0), writes=[("TPh", P), ("TP", P)])
                    else:
                        add("act", I("activation", out=TP[:, P, 0:2], in_=TP[:, prevP, 512:514], func=AF.Copy),
                            reads=[("TP", prevP)], writes=[("TPh", P), ("TP", P)])
                    add("dve", I("tensor_tensor", out=TP[:, P, 2:514], in0=TP[:, cgs, 0:TT], in1=PS[banks[2]][:], op=ALU.mult),
                        reads=[("TP", cgs), ("ps", banks[2])], writes=[("TP", P)])
                    acc = tp_rot.next()
                    add("act", I("activation", out=TP[:, acc, 0:TT], in_=TP[:, P, 2:514], func=AF.Identity,
                                 scale=SM[:, SM_CW + 2 * 4 + q:SM_CW + 2 * 4 + q + 1]),
                        reads=[("TP", P), "SM"], writes=[("TP", acc)])
                    add("dve", I("scalar_tensor_tensor", out=TP[:, acc, 0:TT], in0=TP[:, P, 1:513],
                                 scalar=SM[:, SM_CW + 1 * 4 + q:SM_CW + 1 * 4 + q + 1], in1=TP[:, acc, 0:TT], op0=ALU.mult, op1=ALU.add),
                        reads=[("TP", P), ("TPh", P), ("TP", acc), "SM"], writes=[("TP", acc)])
                    add("dve", I("scalar_tensor_tensor", out=TP[:, acc, 0:TT], in0=TP[:, P, 0:512],
                                 scalar=SM[:, SM_CW + 0 * 4 + q:SM_CW + 0 * 4 + q + 1], in1=TP[:, acc, 0:TT], op0=ALU.mult, op1=ALU.add),
                        reads=[("TP", P), ("TPh", P), ("TP", acc), "SM"], writes=[("TP", acc)])
                    add("dve", I("tensor_tensor", out=AB[:, 4 + q, tsl(t)], in0=TP[:, acc, 0:TT], in1=PS[banks[0]][:], op=ALU.mult),
                        reads=[("TP", acc), ("ps", banks[0])], writes=[("A", 4 + q, t)])
                    prevP = P
            if hf == 0:
                pass

    def ab_tail(l, s):
        return lambda t: outproj_gen(l, s, t, lambda dp: AB_OSLOT[dp], 8, psA)

    Hf = H[:].rearrange("p a b -> p (a b)")
    HFv = Hf32[:, 0:4224].rearrange("p (c k) -> p c k", k=528)
    SAv = Hf32[:, 4224:5280].rearrange("p (c k) -> p c k", k=528)
    SBv = Hf32[:, 5280:6336].rearrange("p (c k) -> p c k", k=528)
    SCv = Hf32[:, 6336:7392].rearrange("p (c k) -> p c k", k=528)
    PPs = Hf[:, 14784:15808].rearrange("p (c t) -> p c t", t=TT)
    ALLH = [("H", k, t) for k in range(8) for t in range(NT)]
    WENG = "pool"

    def pm_load():
        load_piece(("pool",), win_rot.next())

    WMb = WM[:].rearrange("p a b -> p (a b)").bitcast(BF16)
    GTv = [WMb[:, j * 1024:(j + 1) * 1024] for j in range(5)]
    GTK = [[("GT", j), ("WM", j // 2)] for j in range(5)]
    BANDv = [PMC[:, i, :] for i in range(4)]
    BPREVv = [PMC[:, 4 + i, :] for i in range(4)]
    BAND0v = [PMC[:, 8 + i, :] for i in range(4)]

    INVC_KEYS = [("INVC", k) for k in range(16)]

    def pm_prefetch():
        pass

    def pm_consts():
        for k in range(16):
            add("dve", I("memset", INVC[:, k:k + 1], 1.0 / (k + 1)), writes=[("INVC", k)])
        onesf = TP[:, 0, 0:128]
        eye = TP[:, 1, 0:128]
        add("dve", I("memset", onesf, 1.0), writes=[("TP", 0)])
        add("pool", I("affine_select", out=eye, in_=onesf, pattern=[[1, 128]], compare_op=ALU.is_equal, fill=0.0,
                      base=0, channel_multiplier=-1), reads=[("TP", 0)], writes=[("TP", 1)])
        for i in range(4):
            w = 2 ** (i + 1)
            b1 = tp_rot.next()
            B1 = TP[:, b1, 0:128]
            add("pool", I("affine_select", out=B1, in_=onesf, pattern=[[1, 128]], compare_op=ALU.is_ge, fill=0.0,
                          base=0, channel_multiplier=-1), reads=[("TP", 0)], writes=[("TP", b1)])
            add("pool", I("affine_select", out=B1, in_=B1, pattern=[[-1, 128]], compare_op=ALU.is_ge, fill=0.0,
                          base=w - 1, channel_multiplier=1), reads=[("TP", b1)], writes=[("TP", b1)])
            add("dve", I("scalar_tensor_tensor", out=BANDv[i], in0=eye, scalar=-float(w), in1=B1, op0=ALU.mult, op1=ALU.add),
                reads=[("TP", 1), ("TP", b1)], writes=[("BAND", i)])
            add("pool", I("affine_select", out=BPREVv[i], in_=onesf, pattern=[[-1, 128]], compare_op=ALU.is_ge, fill=0.0,
                          base=-(129 - w), channel_multiplier=1), reads=[("TP", 0)], writes=[("BPREV", i)])
            v = tp_rot.next()
            V = TP[:, v, 0:16]
            add("dve", I("memset", V, 0.0), writes=[("TP", v)])
            for t in range(w - 1):
                add("dve", I("memset", TP[:, v, t:t + 1], float(w - 1 - t)), reads=[("TP", v)], writes=[("TPc", v, t)])
            add("dve", I("tensor_tensor", out=V, in0=V, in1=TP[:, 1, 0:16], op=ALU.mult),
                reads=[("TP", v), ("TP", 1)] + [("TPc", v, t) for t in range(w - 1)], writes=[("TP", v)])
            add("dve", I("tensor_tensor", out=BAND0v[i][:, 0:16], in0=V, in1=BANDv[i][:, 0:16], op=ALU.add),
                reads=[("TP", v), ("BAND", i)], writes=[("BAND0", i)])
            add("dve", I("tensor_copy", BAND0v[i][:, 16:128], BANDv[i][:, 16:128]), reads=[("BAND", i)], writes=[("BAND0b", i)])
            add("dve", I("tensor_scalar", out=CORR[:, i, :], in0=INVC[:], scalar1=-1.0 / w, scalar2=None, op0=ALU.add),
                reads=INVC_KEYS, writes=[("CORR", i)])

    def pm_core_gen(l, s, t):
        o = (l * 3 + s) * 8
        psl = piece_slot[("pool",)]
        for n in range(4):
            g = 4 * t + n
            j = g % 5
            tok = slice(t * TT + n * 128, t * TT + (n + 1) * 128)
            banks = [psA.next(), psA.next()]
            for i in range(4):
                bk = banks[i // 2]
                col = (i % 2) * 256
                for kc in range(2):
                    add("pe", I("matmul", PS[bk][:, col:col + 256], H[:, 2 * i + kc, tok], SL[:, psl, (i * 2 + kc) * 256:(i * 2 + kc + 1) * 256],
                                start=(kc == 0), stop=(kc == 1)),
                        reads=[("SL", psl), Hk(2 * i + kc, t)], writes=[("ps", bk)])
            for hh in range(2):
                add("act", I("activation", out=GTv[j][:, hh * 512:(hh + 1) * 512], in_=PS[banks[hh]][:], func=AF.Copy),
                    reads=[("ps", banks[hh])], writes=[("GTh", j, hh)] + GTK[j])
            yield
        for dch in range(NC):
            i = dch // 2
            w = 2 ** (i + 1)
            bk = psA.next()
            for n in range(4):
                g = 4 * t + n
                j = g % 5
                first = (g == 0)
                rd = [("GTh", j, dch // 4)] + GTK[j]
                add("pe", I("matmul", PS[bk][:, n * 128:(n + 1) * 128], GTv[j][:, dch * 128:(dch + 1) * 128],
                            BAND0v[i] if first else BANDv[i], start=True, stop=first),
                    reads=rd + ([("BAND0", i), ("BAND0b", i)] if first else [("BAND", i)]), writes=[("ps", bk)])
                if not first:
                    jp = (g - 1) % 5
                    add("pe", I("matmul", PS[bk][:, n * 128:(n + 1) * 128], GTv[jp][:, dch * 128:(dch + 1) * 128], BPREVv[i],
                                start=False, stop=True),
                        reads=[("GTh", jp, dch // 4), ("BPREV", i)] + GTK[jp], writes=[("ps", bk)])
            add("dve", I("scalar_tensor_tensor", out=X[:, dch, tsl(t)], in0=PS[bk][:], scalar=GH[:, o + dch:o + dch + 1],
                         in1=X[:, dch, tsl(t)], op0=ALU.mult, op1=ALU.add),
                reads=[("ps", bk), ("GH", l, s), Xk(dch, t)], writes=[Xk(dch, t)])
            if t == 0:
                tmpi = tp_rot.next()
                add("dve", I("tensor_tensor", out=TP[:, tmpi, 0:w - 1], in0=PS[bk][:, 0:w - 1], in1=CORR[:, i, 0:w - 1], op=ALU.mult),
                    reads=[("ps", bk), ("CORR", i)], writes=[("TP", tmpi)])
                add("dve", I("scalar_tensor_tensor", out=X[:, dch, 0:w - 1], in0=TP[:, tmpi, 0:w - 1], scalar=GS[:, dch:dch + 1],
                             in1=X[:, dch, 0:w - 1], op0=ALU.mult, op1=ALU.add),
                    reads=[("TP", tmpi), "GS", Xk(dch, t)], writes=[Xk(dch, t)])
            yield

    def pm_stage2(tail_fn, head_fn):
        for k in range(NT + 3):
            gl = []
            if k < NT:
                gl.append((tail_fn(k), 1))
            if 0 <= k - 1 < NT:
                gl.append((head_lag_gen(1, 1, k - 1, ps6, SQ_PM, (), True), 2))
            if 0 <= k - 2 < NT:
                gl.append((pm_core_gen(1, 1, k - 2), 2))
            h = k - 3
            if head_fn is not None and 0 <= h < NT:
                gl.append((head_fn(h), 2))
                if h == NT - 1:
                    gl.append((early_wave_gen(1, 1, 2, 3), 1))
            stepper(gl)

    SQ_PM = [(VH[:, 1, :], ("VH", 1)), (VH[:, 2, :], ("VH", 2)), (VH[:, 3, :], ("VH", 3)), (JUNK[:, 0:TT], "JUNK")]
    SQ_HD = [(SQ[:, 0, :], ("SQ", 0)), (SQ[:, 1, :], ("SQ", 1)), (SQ[:, 2, :], ("SQ", 2)), (VH[:, 0, :], ("VH", 0))]

    def stats_lag_gen(t, res, rot, tiles, lag=3, sq_pool=False):
        b = rot.next()
        n = len(tiles)
        for c in range(NC + lag):
            if c < NC:
                ap, k = tiles[c % n]
                if sq_pool:
                    add("pool", I("tensor_tensor", out=ap, in0=X[:, c, tsl(t)], in1=X[:, c, tsl(t)], op=ALU.mult),
                        reads=[Xk(c, t)], writes=[k])
                else:
                    add("act", I("activation", out=ap, in_=X[:, c, tsl(t)], func=AF.Square), reads=[Xk(c, t)], writes=[k])
            cc = c - lag
            if cc >= 0:
                ap, k = tiles[cc % n]
                add("pe", I("matmul", PS[b][:], ONES[:], ap, start=(cc == 0), stop=(cc == NC - 1)), reads=[k, "ONES"], writes=[("ps", b)])
            yield
        res.append(rstd_from(b))
        yield

    def head_lag_gen(l, s, t, rot, tiles, pool_chunks=(1, 4, 7), sq_pool=False):
        res = []
        yield from stats_lag_gen(t, res, rot, tiles, 3, sq_pool)
        yield from apply_gen(l, s, t, res[0], "H", pool_chunks)

    WBUF = {"A": (SAv, ("SA",)), "B": (SBv, ("SB",)), "C": (SCv, ("SC",))}
    WPLAN = {0: "A", 1: "BC", 2: "BAB", 3: "ACAC"}

    def pm_gen(l, s, t):
        o = (l * 3 + s) * 8
        psl = piece_slot[("pool",)]
        if t == 0:
            add("dve", I("memset", HFv[:, :, 0:16], 0.0), reads=[INDONE] + ALLH, writes=[("HFh",)])
        else:
            add("act", I("activation", out=HFv[:, :, 0:16], in_=HALO[:], func=AF.Copy), reads=["HALO", INDONE] + ALLH, writes=[("HFh",)])
        yield
        res = []
        yield from stats_lag_gen(t, res, ps6, SQ_PM)
        rs = res[0]
        for c in range(NC):
            add("dve", I("tensor_tensor", out=HFv[:, c, 16:16 + TT], in0=X[:, c, tsl(t)], in1=TP[:, rs, 0:TT], op=ALU.mult),
                reads=[Xk(c, t), ("TP", rs), INDONE] + ALLH, writes=[("HF", c)])
            yield
        hf_all = [("HF", c) for c in range(NC)] + [("HFh",)]
        if t < NT - 1:
            add("act", I("activation", out=HALO[:], in_=HFv[:, :, 512:528], func=AF.Copy), reads=hf_all + ALLH, writes=["HALO"])
        for i in range(4):
            cs = slice(2 * i, 2 * i + 2)
            hk = [("HF", 2 * i), ("HF", 2 * i + 1), ("HFh",)]
            plan_i = WPLAN[i]
            cur, curk = WBUF[plan_i[0]]
            add(WENG, I("tensor_tensor", out=cur[:, :, 1:528], in0=HFv[:, cs, 1:528], in1=HFv[:, cs, 0:527], op=ALU.add),
                reads=hk + [INDONE] + ALLH, writes=[curk])
            sh = 2
            lo = 1
            for nb in plan_i[1:]:
                oth, othk = WBUF[nb]
                lo2 = lo + sh
                add(WENG, I("tensor_tensor", out=oth[:, :, lo2:528], in0=cur[:, :, lo2:528], in1=cur[:, :, lo2 - sh:528 - sh], op=ALU.add),
                    reads=[curk, INDONE] + ALLH, writes=[othk])
                cur, curk = oth, othk
                lo = lo2
                sh *= 2
            w = 2 ** (i + 1)
            yield
            PP = PPs
            ppk = ("PP",)
            add("dve", I("scalar_tensor_tensor", out=PP[:], in0=cur[:, :, 16:528], scalar=1.0 / w, in1=HFv[:, cs, 16:528],
                         op0=ALU.mult, op1=ALU.subtract),
                reads=[curk, INDONE] + hk + ALLH, writes=[ppk])
            if t == 0:
                for cc in range(2):
                    tmpi = tp_rot.next()
                    add("dve", I("tensor_tensor", out=TP[:, tmpi, 0:w - 1], in0=cur[:, cc, 16:16 + w - 1], in1=INVC[:, 0:w - 1], op=ALU.mult),
                        reads=[curk, "INVC"], writes=[("TP", tmpi)])
                    add("dve", I("tensor_tensor", out=PP[:, cc, 0:w - 1], in0=TP[:, tmpi, 0:w - 1],
                                 in1=HFv[:, 2 * i + cc, 16:16 + w - 1], op=ALU.subtract),
                        reads=[("TP", tmpi), ppk] + hk + ALLH, writes=[ppk])
            yield
            for oh in range(2):
                b = psA.next()
                dch = 2 * i + oh
                for kc in range(2):
                    add("pe", I("matmul", PS[b][:], SL[:, psl, (i * 2 + kc) * 256 + oh * 128:(i * 2 + kc) * 256 + (oh + 1) * 128],
                                PP[:, kc, :], start=(kc == 0), stop=(kc == 1)),
                        reads=[("SL", psl), ppk] + ALLH, writes=[("ps", b)])
                add("dve", I("scalar_tensor_tensor", out=X[:, dch, tsl(t)], in0=PS[b][:], scalar=GH[:, o + dch:o + dch + 1],
                             in1=X[:, dch, tsl(t)], op0=ALU.mult, op1=ALU.add),
                    reads=[("ps", b), ("GH", l, s), Xk(dch, t)], writes=[Xk(dch, t)])
            yield

    RL = role_of(1, 2)
    Ov = [RL.hbuf[:, 4 * i:4 * i + 4, :].bitcast(F32).rearrange("p a b -> p (a b)").rearrange("p (c k) -> p c k", k=TT) for i in range(2)]
    O_KEYS = [[(RL.hn, c, t) for c in range(4 * i, 4 * i + 4) for t in range(NT)] for i in range(2)]

    def final_gen(t):
        res = []
        yield from stats_gen(t, res)
        yield from final_apply_gen(t, res[0])

    def final_apply_gen(t, rs):
        ob = t % 2
        for c in range(NC):
            add("dve", I("scalar_tensor_tensor", out=Ov[ob][:, c, :], in0=X[:, c, tsl(t)], scalar=SM[:, SM_FG + c:SM_FG + c + 1],
                         in1=TP[:, rs, 0:TT], op0=ALU.mult, op1=ALU.mult),
                reads=[Xk(c, t), "SM", ("TP", rs)], writes=[("O", ob, c)] + O_KEYS[ob])
            yield
        add("sp", I("dma_start", out=outT_v[:, :, tsl(t)], in_=Ov[ob][:]), reads=[("O", ob, c) for c in range(NC)] + O_KEYS[ob], dma=("OUT", ob))

    def debug_store():
        for t in range(NT):
            add("sp", I("dma_start", out=outT_v[:, :, tsl(t)], in_=X[:, :, tsl(t)]),
                reads=[Xk(c, t) for c in range(NC)], dma=("OUT", t))
        add("sp", None, writes=[Xk(c, t) for c in range(NC) for t in range(NT)])

    add("sp", I("dma_start", out=SM[:], in_=small_d), writes=["SM"], dma="SM")
    add("sp", I("dma_start", out=CB[:], in_=cb_d), writes=["CB"], dma="CB")
    add("act", I("activation", out=CB[:], in_=CB[:], func=AF.Silu), reads=["CB"], writes=["CB"])
    add("dve", I("memset", ONES[:], 1.0), writes=["ONES"])
    add("dve", I("memset", EPSB[:], EPS), writes=["EPSB"])
    add("dve", I("memset", MHALF[:], -0.5), writes=["MHALF"])
    if n_sub_run >= 5:
        pm_consts()
    add("sp", I("dma_start", out=X[:, :, tsl(0)], in_=xT_v[:, :, tsl(0)]), writes=[Xk(c, 0) for c in range(NC)], dma=("X", 0))
    ffn_prefetch(0, 0, 1)
    for q in range(16):
        mod_staged[q] = mod_dma(q)
    for t in range(1, NT):
        add("sp", I("dma_start", out=X[:, :, tsl(t)], in_=xT_v[:, :, tsl(t)]), writes=[Xk(c, t) for c in range(NC)], dma=("X", t))
    mod_finalize_norm(0, 0)

    seq = [(0, 0), (0, 1), (0, 2), (1, 0), (1, 1), (1, 2)][:n_sub_run]
    last = len(seq) - 1

    def body_of(idx):
        l, s = seq[idx]
        if s == 0:
            ffn_body(l, s, 0, 72 if l == 0 else 144, wave_cb=startup_cb if idx == 0 else None)
        elif s == 2:
            ffn_body(l, s, 1, 144)
        else:
            ab_body(l, s)

    def tail_of(idx):
        l, s = seq[idx]
        if s == 1:
            return ab_tail(l, s)
        return ffn_tail(l, s)

    def prefetch_of(idx):
        l, s = seq[idx]
        mod_finalize_norm(l, s)
        if s == 0:
            ffn_prefetch(l, 0)
        elif s == 2:
            ffn_prefetch(l, 1, 2 if l == 1 else 3)
        elif l == 0:
            ab_prefetch()
        else:
            pm_prefetch()
        if s == 1:
            mod_finalize_gate(l, s)

    st_rs = {}

    def st_stats(t):
        res = []
        drain(stats_gen(t, res, None, True))
        st_rs[t] = res[0]

    def st_apply(t):
        drain(apply_gen(0, 0, t, st_rs[t], "H"))

    def startup_cb(t):
        if t + 1 < NT:
            st_apply(t + 1)
        if t + 2 < NT:
            st_stats(t + 2)

    st_stats(0)
    for j in GROUPS[0][1:3]:
        load_piece(("win", 0, 0, j), None, [("MODT", 11)])
    st_apply(0)
    st_stats(1)

    idx = 0
    while idx <= last:
        l, s = seq[idx]
        if (l, s) == (1, 1):
            idx += 1
            continue
        body_of(idx)
        stages = [tail_of(idx)]
        weights = [1]
        nxt = idx + 1
        if nxt <= last:
            ln, sn = seq[nxt]
            prefetch_of(nxt)
            if (ln, sn) == (1, 1):
                pm_load()
                stages.append(lambda t: pm_gen(1, 1, t))
                weights.append(3)
                if nxt + 1 <= last:
                    prefetch_of(nxt + 1)
                    ln2, sn2 = seq[nxt + 1]
                    stages.append(lambda t, ln2=ln2, sn2=sn2: head_lag_gen(ln2, sn2, t, ps7, SQ_HD, (), True))
                    weights.append(2)
            else:
                boundary(stages[0], lambda t, rs, ln=ln, sn=sn: apply_gen(ln, sn, t, rs, "H"))
                idx += 1
                continue
        elif run_final and n_sub_run == 6:
            boundary(stages[0], final_apply_gen)
            idx += 1
            continue
        if len(stages) >= 2:
            pm_stage2(stages[0], stages[2] if len(stages) == 3 else None)
        else:
            run_stages(stages, weights)
        idx += 1
    if run_final and n_sub_run == 6:
        add("sp", None, writes=O_KEYS[0] + O_KEYS[1] + [("O", ob, c) for ob in range(2) for c in range(NC)])
    else:
        debug_store()

    with nc.Block() as block:
        Sd.emit(nc, block)
    nc._sched = Sd
    return nc


_CACHE = {}


def kernel(x, c, norm_g, w_mod, b_mod, w_ffn_in, w_ffn_out, ab_w_in, ab_norm_v, ab_w_s, ab_b_s,
           ab_conv_w, ab_w_out, pool_w_grp, pool_scale, final_g, _n_sub_run=N_SUB_RUN, _run_final=RUN_FINAL):
    inp = dict(x=x, c=c, norm_g=norm_g, w_mod=w_mod, b_mod=b_mod, w_ffn_in=w_ffn_in, w_ffn_out=w_ffn_out,
               ab_w_in=ab_w_in, ab_norm_v=ab_norm_v, ab_w_s=ab_w_s, ab_b_s=ab_b_s, ab_conv_w=ab_conv_w,
               ab_w_out=ab_w_out, pool_w_grp=pool_w_grp, pool_scale=pool_scale, final_g=final_g)
    inp = {k: np.asarray(v, dtype=np.float32) for k, v in inp.items()}
    key = (_n_sub_run, _run_final)
    if key not in _CACHE:
        _CACHE[key] = build_nc(_n_sub_run, _run_final)
    nc = _CACHE[key]
    wp = build_pieces(inp)
    small = build_small(inp)
    wmod = build_wmod(inp)
    wst = np.ascontiguousarray(inp["ab_w_s"][0].transpose(2, 0, 1).reshape(128, 512))
    in_maps = []
    for b in range(8):
        in_maps.append({
            "xT": np.ascontiguousarray(inp["x"][b].T),
            "cb": np.ascontiguousarray(np.broadcast_to(inp["c"][b][None, :], (128, D))),
            "small": small, "wst": wst, "wmod": wmod, "wp": wp,
        })
    res = run_bass_kernel_spmd(nc, in_maps, core_ids=list(range(8)))
    out = np.stack([np.ascontiguousarray(r["outT"].T) for r in res.results], axis=0)
    return out.astype(np.float32)
```
